# Optimizing a Trainium2 kernel written in Bass

```python
import math
import jax, jax.numpy as jnp
from jax import lax
import numpy as np

D_MODEL = 2048
BATCH = 4
SEQ = 2048
DEPTH = 2

HG_HEADS = 8
HG_KDIM = 128
HG_VDIM = 128
HG_WIDTH = HG_HEADS * HG_VDIM
HG_CHUNK = 64
POOL_WINDOWS = (2, 4, 8, 16)
POOL_GROUPS = len(POOL_WINDOWS)
POOL_WIDTH = 1024
POOL_GDIM = POOL_WIDTH // POOL_GROUPS
SG_GROUPS = 8
SG_WIDTH = 1024
SG_GDIM = SG_WIDTH // SG_GROUPS
SG_CHUNK = 128
N_BRANCH = 3
D_FF = 5632
CONV_W = 3
LN_EPS = 1e-5
RMS_EPS = 1e-6
DN_ALPHA = (2 * DEPTH) ** 0.25
DN_BETA = (8 * DEPTH) ** -0.25
IN_WIDTHS = (HG_HEADS * HG_KDIM, HG_HEADS * HG_KDIM, HG_WIDTH, HG_WIDTH,
             POOL_WIDTH, SG_WIDTH, SG_WIDTH, N_BRANCH * D_MODEL)
D_IN = sum(IN_WIDTHS)
IN_SPLITS = tuple(int(c) for c in np.cumsum(IN_WIDTHS)[:-1])

kernel_name = "hybrid_hgrn2_pool_sgu_deepnorm"


def layer_norm(x, g, b):
    xf = x.astype(jnp.float32)
    mu = jnp.mean(xf, axis=-1, keepdims=True)
    var = jnp.mean(jnp.square(xf - mu), axis=-1, keepdims=True)
    y = (xf - mu) * lax.rsqrt(var + LN_EPS) * g.astype(jnp.float32) + b.astype(jnp.float32)
    return y.astype(x.dtype)


def hgrn2_mixer(q, f_raw, v, og, lb, norm_g):
    B, S, _ = v.shape
    dt = v.dtype
    nc = S // HG_CHUNK
    f32 = jnp.float32
    lbf = lb.astype(f32)
    fr = f_raw.astype(f32)
    qf = jax.nn.silu(q.astype(f32))
    log_f = jnp.logaddexp(jnp.log(lbf), jnp.log1p(-lbf) + jax.nn.log_sigmoid(fr))
    kf = (1.0 - lbf) * jax.nn.sigmoid(-fr)
    vf = v.astype(f32)

    def to_chunks(t, d):
        return t.reshape(B, nc, HG_CHUNK, HG_HEADS, d).transpose(1, 0, 3, 2, 4)

    qc, kc, lfc = to_chunks(qf, HG_KDIM), to_chunks(kf, HG_KDIM), to_chunks(log_f, HG_KDIM)
    vc = to_chunks(vf, HG_VDIM)
    causal = jnp.tril(jnp.ones((HG_CHUNK, HG_CHUNK), dtype=bool))

    def step(state, inp):
        q_c, k_c, lf_c, v_c = inp
        b = jnp.cumsum(lf_c, axis=2)
        b_last = b[:, :, -1:, :]
        o_inter = jnp.einsum('bhck,bhkv->bhcv', q_c * jnp.exp(b), state)
        diff = b[:, :, :, None, :] - b[:, :, None, :, :]
        decay = jnp.exp(jnp.where(causal[:, :, None], diff, -jnp.inf))
        scores = jnp.einsum('bhtk,bhtsk,bhsk->bhts', q_c, decay, k_c)
        o = o_inter + jnp.einsum('bhts,bhsv->bhtv', scores, v_c)
        new_state = (jnp.exp(b_last[:, :, 0, :])[..., None] * state
                     + jnp.einsum('bhsk,bhsv->bhkv', k_c * jnp.exp(b_last - b), v_c))
        return new_state, o

    s0 = jnp.zeros((B, HG_HEADS, HG_KDIM, HG_VDIM), f32)
    _, o = lax.scan(step, s0, (qc, kc, lfc, vc))
    o = o.transpose(1, 0, 3, 2, 4).reshape(B, S, HG_HEADS, HG_VDIM)
    o = o * lax.rsqrt(jnp.mean(jnp.square(o), axis=-1, keepdims=True) + RMS_EPS)
    o = o.reshape(B, S, HG_WIDTH) * norm_g.astype(f32) * jax.nn.silu(og.astype(f32))
    return o.astype(dt)


def pool_mixer(p, w_grp, scale):
    B, S, _ = p.shape
    dt = p.dtype
    pf = p.astype(jnp.float32).reshape(B, S, POOL_GROUPS, POOL_GDIM)
    cs = jnp.cumsum(pf, axis=1)
    count_base = jnp.arange(1, S + 1, dtype=jnp.float32)
    outs = []
    for g, w in enumerate(POOL_WINDOWS):
        c = cs[:, :, g]
        lagged = jnp.pad(c, ((0, 0), (w, 0), (0, 0)))[:, :S]
        mean = (c - lagged) / jnp.minimum(count_base, float(w))[None, :, None]
        outs.append(mean - pf[:, :, g])
    pooled = jnp.stack(outs, axis=2)
    y = jnp.einsum('bsgc,gcd->bsgd', pooled, w_grp.astype(jnp.float32)).reshape(B, S, POOL_WIDTH)
    return (y * scale.astype(jnp.float32)).astype(dt)


def sgu_mixer(u, v, ln_g, ln_b, w_s, b_s):
    B, S, _ = u.shape
    u = jax.nn.gelu(u)
    v = layer_norm(jax.nn.gelu(v), ln_g, ln_b)
    nc = S // SG_CHUNK
    vc = v.reshape(B, nc, SG_CHUNK, SG_GROUPS, SG_GDIM)
    w = w_s * jnp.tril(jnp.ones((SG_CHUNK, SG_CHUNK), dtype=w_s.dtype))
    mixed = jnp.einsum('gts,bnsgd->bntgd', w, vc) + b_s.T[None, None, :, :, None]
    return u * mixed.reshape(B, S, SG_WIDTH)


def conv_ffn(x, w_up, conv_w, conv_b, w_down):
    h = x @ w_up
    S = h.shape[1]
    hp = jnp.pad(h, ((0, 0), (CONV_W - 1, 0), (0, 0)))
    acc = conv_b + conv_w[0] * hp[:, 0:S]
    for j in range(1, CONV_W):
        acc = acc + conv_w[j] * hp[:, j:j + S]
    a, b = jnp.split(acc, 2, axis=-1)
    return (jax.nn.silu(a) * b) @ w_down


def setup_inputs(seed: int = 0) -> dict:
    key = jax.random.key(seed)
    ks = jax.random.split(key, 24)
    L = DEPTH

    def nrm(k, shape, scale):
        return jax.random.normal(k, shape, jnp.float32) * scale

    return {
        "x": nrm(ks[0], (BATCH, SEQ, D_MODEL), 1.0),
        "w_in": nrm(ks[1], (L, D_MODEL, D_IN), D_MODEL ** -0.5),
        "hg_lower_bounds": nrm(ks[2], (L, HG_HEADS * HG_KDIM), 0.1),
        "hg_norm_g": 1.0 + nrm(ks[3], (L, HG_WIDTH), 0.02),
        "pool_w": nrm(ks[4], (L, POOL_GROUPS, POOL_GDIM, POOL_GDIM), POOL_GDIM ** -0.5),
        "pool_scale": 1.0 + nrm(ks[5], (L, POOL_WIDTH), 0.02),
        "sg_ln_g": 1.0 + nrm(ks[6], (L, SG_WIDTH), 0.02),
        "sg_ln_b": nrm(ks[7], (L, SG_WIDTH), 0.02),
        "sg_w": nrm(ks[8], (L, SG_GROUPS, SG_CHUNK, SG_CHUNK), 0.5 * SG_CHUNK ** -0.5),
        "sg_b": 1.0 + nrm(ks[9], (L, SG_GROUPS, SG_CHUNK), 0.02),
        "w_hg_proj": nrm(ks[10], (L, HG_WIDTH, D_MODEL), DN_BETA * HG_WIDTH ** -0.5),
        "w_pool_proj": nrm(ks[11], (L, POOL_WIDTH, D_MODEL), DN_BETA * POOL_WIDTH ** -0.5),
        "w_sg_proj": nrm(ks[12], (L, SG_WIDTH, D_MODEL), DN_BETA * SG_WIDTH ** -0.5),
        "w_out": nrm(ks[13], (L, D_MODEL, D_MODEL), DN_BETA * D_MODEL ** -0.5),
        "ln1_g": 1.0 + nrm(ks[14], (L, D_MODEL), 0.02),
        "ln1_b": nrm(ks[15], (L, D_MODEL), 0.02),
        "w_up": nrm(ks[16], (L, D_MODEL, 2 * D_FF), DN_BETA * D_MODEL ** -0.5),
        "conv_w": nrm(ks[17], (L, CONV_W, 2 * D_FF), CONV_W ** -0.5),
        "conv_b": nrm(ks[18], (L, 2 * D_FF), 0.02),
        "w_down": nrm(ks[19], (L, D_FF, D_MODEL), DN_BETA * D_FF ** -0.5),
        "ln2_g": 1.0 + nrm(ks[20], (L, D_MODEL), 0.02),
        "ln2_b": nrm(ks[21], (L, D_MODEL), 0.02),
    }


def reference(x, w_in, hg_lower_bounds, hg_norm_g, pool_w, pool_scale, sg_ln_g, sg_ln_b,
              sg_w, sg_b, w_hg_proj, w_pool_proj, w_sg_proj, w_out, ln1_g, ln1_b,
              w_up, conv_w, conv_b, w_down, ln2_g, ln2_b):
    B, S, D = x.shape
    lb_all = jnp.cumsum(jax.nn.softmax(hg_lower_bounds.astype(jnp.float32), axis=0), axis=0)
    lb_all = lb_all - lb_all[0:1]
    for l in range(DEPTH):
        z = x @ w_in[l]
        q, f_raw, i_in, og, p, u, v, gr = jnp.split(z, IN_SPLITS, axis=-1)
        y_hg = hgrn2_mixer(q, f_raw, i_in, og, lb_all[l], hg_norm_g[l])
        y_pool = pool_mixer(p, pool_w[l], pool_scale[l])
        y_sg = sgu_mixer(u, v, sg_ln_g[l], sg_ln_b[l], sg_w[l], sg_b[l])
        gates = jax.nn.sigmoid(gr.astype(jnp.float32)).astype(x.dtype).reshape(B, S, N_BRANCH, D)
        merged = (gates[:, :, 0] * (y_hg @ w_hg_proj[l])
                  + gates[:, :, 1] * (y_pool @ w_pool_proj[l])
                  + gates[:, :, 2] * (y_sg @ w_sg_proj[l]))
        mix = merged @ w_out[l]
        x = layer_norm(DN_ALPHA * x + mix, ln1_g[l], ln1_b[l])
        x = layer_norm(DN_ALPHA * x + conv_ffn(x, w_up[l], conv_w[l], conv_b[l], w_down[l]),
                       ln2_g[l], ln2_b[l])
    return x
```

```python
import os
import numpy as np
import concourse.bass as bass
import concourse.mybir as mybir
from concourse.bass_utils import run_bass_kernel_spmd

F32 = mybir.dt.float32
BF16 = mybir.dt.bfloat16
AF = mybir.ActivationFunctionType
ALU = mybir.AluOpType

D = 2048
S = 2048
NB = 4
DEPTH = 2
TB = 512
NTB = S // TB
NL = 1
NSTEP = NTB + 1
NCORES = 8
KC = D // 128
D_IN = 13312
D_FF = 5632
NJ = D_FF // 128
ALPHA = float((2 * DEPTH) ** 0.25)
LN_EPS = 1e-5
RMS_EPS = 1e-6
C_Q, C_F, C_I, C_OG, C_P, C_U, C_V, C_G = 0, 1024, 2048, 3072, 4096, 5120, 6144, 7168
NSLOT = 4
SLOT_ELEMS = 16 * 256

V_LB = 0
V_NG = V_LB + 16
V_PS = V_NG + 16
V_L1G = V_PS + 16
V_L1B = V_L1G + 32
V_L2G = V_L1B + 32
V_L2B = V_L2G + 32
V_CW = V_L2B + 32
V_CB = V_CW + 2 * 3 * 88
V_FL = V_CB + 2 * 88
V_XA = V_FL + 2 + NSTEP
NV = V_XA + NSTEP
NCONST = 128 + 128 + TB + 128


class Res:
    __slots__ = ("w", "r", "name")

    def __init__(self, name=""):
        self.w = None
        self.r = []
        self.name = name


class Sched:
    def __init__(self, nc, es):
        self.nc = nc
        self.eng = {"pe": nc.tensor, "act": nc.scalar, "dve": nc.vector, "pool": nc.gpsimd, "sp": nc.sync}
        self.sem = {}
        self.cnt = {}
        self.mult = {}
        for e in ("pe", "act", "dve", "pool"):
            self.sem[e] = es.enter_context(nc.semaphore("c_" + e))
            self.cnt[e] = 0
            self.mult[e] = 1
        self.seen = {e: {} for e in self.eng}
        self.es = es
        self.n_wait = 0

    def stream(self, name):
        self.sem[name] = self.es.enter_context(self.nc.semaphore("d_" + name))
        self.cnt[name] = 0
        self.mult[name] = 16
        return name

    def _deps(self, reads, writes):
        deps = {}
        for r in reads:
            if r.w is not None:
                e, t = r.w
                if deps.get(e, 0) < t:
                    deps[e] = t
        for w in writes:
            if w.w is not None:
                e, t = w.w
                if deps.get(e, 0) < t:
                    deps[e] = t
            for e, t in w.r:
                if deps.get(e, 0) < t:
                    deps[e] = t
        return deps

    def _wait(self, eng, deps):
        seen = self.seen[eng]
        for e, t in deps.items():
            if e == eng and eng == "pe":
                continue
            if seen.get(e, 0) < t:
                self.eng[eng].wait_ge(self.sem[e], t * self.mult[e])
                seen[e] = t
                self.n_wait += 1

    def _mark(self, tick, reads, writes):
        for r in reads:
            r.r.append(tick)
            if len(r.r) > 64:
                best = {}
                for e, t in r.r:
                    if best.get(e, 0) < t:
                        best[e] = t
                r.r = list(best.items())
        for w in writes:
            w.w = tick
            w.r = []

    def op(self, eng, fn, reads=(), writes=()):
        self._wait(eng, self._deps(reads, writes))
        inst = fn(self.eng[eng])
        self.cnt[eng] += 1
        inst.then_inc(self.sem[eng], 1)
        self._mark((eng, self.cnt[eng]), reads, writes)

    def dma(self, q, stream, out, in_, reads=(), writes=()):
        self._wait(q, self._deps(reads, writes))
        inst = self.eng[q].dma_start(out=out, in_=in_)
        self.cnt[stream] += 1
        inst.then_inc(self.sem[stream], 16)
        self._mark((stream, self.cnt[stream]), reads, writes)

    def barrier(self, full=False):
        for e in (("pe",) if full else ()) + ("act", "dve", "pool", "sp"):
            deps = {o: self.cnt[o] for o in ("pe", "act", "dve", "pool") if self.cnt[o] > 0}
            self._wait(e, deps)

    def wait_all(self, eng, streams):
        self._wait(eng, {s: self.cnt[s] for s in streams if self.cnt[s] > 0})


def strip_seq(l):
    seq = []
    for hp in range(4):
        for nm, c0 in (("q", C_Q), ("f", C_F), ("i", C_I), ("og", C_OG)):
            seq.append((nm, "w_in", 0, D, c0 + hp * 256, 256))
    for g in range(4):
        seq.append(("p", "w_in", 0, D, C_P + g * 256, 256))
    for s in range(4):
        seq.append(("v", "w_in", 0, D, C_V + s * 256, 256))
    for s in range(4):
        seq.append(("u", "w_in", 0, D, C_U + s * 256, 256))
    for dp in range(8):
        for b, wn in enumerate(("w_hg_proj", "w_pool_proj", "w_sg_proj")):
            seq.append(("g%d" % b, "w_in", 0, D, C_G + b * D + dp * 256, 256))
            seq.append(("P%d" % b, wn, 0, 1024, dp * 256, 256))
    for dp in range(8):
        seq.append(("wo", "w_out", 0, D, dp * 256, 256))
    for jp in range(NJ // 2):
        seq.append(("ua", "w_up", 0, D, jp * 256, 256))
        seq.append(("ub", "w_up", 0, D, D_FF + jp * 256, 256))
    for dp in range(8):
        for q4 in range(4):
            seq.append(("wd%d" % q4, "w_down", q4 * 11 * 128, 11 * 128, dp * 256, 256))
    return seq


class Ring:
    def __init__(self, sch, nc, es, wdram):
        self.sch = sch
        self.wdram = wdram
        self.slots = []
        for i in range(NSLOT):
            t = es.enter_context(nc.sbuf_tensor("wslot%d" % i, [128, SLOT_ELEMS], BF16))
            self.slots.append((t, Res("wslot%d" % i), sch.stream("ws%d" % i)))
        self.seq = []
        for step in range(NSTEP):
            for item in strip_seq(0):
                self.seq.append((0,) + item)
        self.issued = 0
        self.pos = 0

    def _issue(self, j):
        l, key, wn, r0, nr, c0, ncol = self.seq[j]
        t, res, st = self.slots[j % NSLOT]
        kc = nr // 128
        src = self.wdram[wn][l, r0:r0 + nr, c0:c0 + ncol].rearrange("(kc p) n -> p kc n", p=128)
        dst = t[:, 0:kc * ncol].rearrange("p (kc n) -> p kc n", n=ncol)
        self.sch.dma("pool", st, dst, src, reads=(), writes=(res,))

    def acquire(self, key):
        j = self.pos
        assert self.seq[j][1] == key, (self.seq[j], key)
        while self.issued < min(len(self.seq), j + NSLOT):
            self._issue(self.issued)
            self.issued += 1
        self.pos += 1
        l, key, wn, r0, nr, c0, ncol = self.seq[j]
        t, res, st = self.slots[j % NSLOT]
        kc = nr // 128
        view = t[:, 0:kc * ncol].rearrange("p (kc n) -> p kc n", n=ncol)
        return view, res


def build_program(debug_stop=None, n_cores=NCORES):
    from contextlib import ExitStack
    nc = bass.Bass("TRN2", target_bir_lowering=False)
    es = ExitStack()
    with es:
        x_d = nc.dram_tensor("x", [S, D], F32, kind="ExternalInput").ap()
        out_d = nc.dram_tensor("out", [S, D], F32, kind="ExternalOutput").ap()
        wdram = {}
        xmode = debug_stop is not None and debug_stop.startswith("X")
        for nm, shp in () if xmode else (("w_in", [NL, D, D_IN]), ("w_hg_proj", [NL, 1024, D]), ("w_pool_proj", [NL, 1024, D]),
                        ("w_sg_proj", [NL, 1024, D]), ("w_out", [NL, D, D]), ("w_up", [NL, D, 2 * D_FF]),
                        ("w_down", [NL, D_FF, D])):
            wdram[nm] = nc.dram_tensor(nm, shp, F32, kind="ExternalInput").ap()
        vecs_d = nc.dram_tensor("vecs", [128, NV], F32, kind="ExternalInput").ap()
        poolw_d = nc.dram_tensor("pool_w", [NL, 4, 256, 256], F32, kind="ExternalInput").ap()
        sgwT_d = nc.dram_tensor("sg_wT", [NL, 128, 8, 128], F32, kind="ExternalInput").ap()
        sgb_d = nc.dram_tensor("sg_b", [1, NL * 1024], F32, kind="ExternalInput").ap()
        sglng_d = nc.dram_tensor("sg_ln_g", [NL, 1024], F32, kind="ExternalInput").ap()
        sglnb_d = nc.dram_tensor("sg_ln_b", [NL, 1024], F32, kind="ExternalInput").ap()
        consts_d = nc.dram_tensor("consts", [128, NCONST], F32, kind="ExternalInput").ap()
        cc_src = nc.dram_tensor("cc_src", [TB, D], F32)
        cc_dst = nc.dram_tensor("cc_dst", [2, TB, D], F32)
        rgroups = [[2 * i, 2 * i + 1] for i in range(n_cores // 2)]
        dbg_d = None
        if debug_stop is not None:
            dbg_d = nc.dram_tensor("dbg", [128, 16 * TB], F32, kind="ExternalOutput").ap()

        sch = Sched(nc, es)
        ring = Ring(sch, nc, es, wdram) if not xmode else None
        st_misc = sch.stream("misc")
        st_x = sch.stream("xin")
        st_out = sch.stream("xout")
        st_ln = sch.stream("sgln")
        st_g = [sch.stream("gin%d" % i) for i in range(2)]
        st_xs = [sch.stream("xin%d" % i) for i in range(4)]
        st_cs = [sch.stream("ccsrc%d" % i) for i in range(2)]
        st_outs = [sch.stream("xout%d" % i) for i in range(2)]
        sch.sem["cc"] = es.enter_context(nc.semaphore("cc_done"))
        sch.cnt["cc"] = 0
        sch.mult["cc"] = 1
        R_ccsrc = [Res("ccsrc%d" % i) for i in range(2)]
        R_ccdst = [Res("ccdst%d" % i) for i in range(2)]

        uniq = {"n": 0}

        def sb(name, shape, dt=F32, stack=es):
            uniq["n"] += 1
            return stack.enter_context(nc.sbuf_tensor("%s_%d" % (name, uniq["n"]), shape, dt))

        xT = sb("xT", [128, KC, TB], F32)
        xTb = sb("xTb", [128, KC, TB], BF16)
        R_xT = [Res("xT%d" % i) for i in range(KC)]
        R_xTb = [Res("xTb%d" % i) for i in range(KC)]
        vecs = sb("vecs_sb", [128, NV], F32)
        R_vecs = Res("vecs")
        consts = sb("consts_sb", [128, NCONST], F32)
        R_consts = Res("consts")
        ident_f = consts[:, 0:128]
        tril_f = consts[:, 128:256]
        scanmask = consts[:, 256:256 + TB]
        corr_tabs = [consts[:, 256 + TB:256 + TB + 64], consts[:, 256 + TB + 64:256 + TB + 128]]
        ident_b = sb("ident_b", [128, 128], BF16)
        ones_b = sb("ones_b", [128, 128], BF16)
        R_cb = Res("constb")
        lb1 = sb("lb", [128, 2, 8], F32)
        oml = sb("oml", [128, 2, 8], F32)
        noml = sb("noml", [128, 2, 8], F32)
        R_lb = Res("lb")
        poolw = sb("poolw", [128, NL, 4 * 2 * 256], BF16)
        R_poolw = Res("poolw")
        sgw = sb("sgw", [128, NL, 8 * 128], BF16)
        R_sgw = Res("sgw")
        sgb = sb("sgb", [1, NL * 1024], BF16)
        R_sgb = Res("sgb")
        S_st = sb("S_st", [128, NL * 8, 128], F32)
        R_S = [[Res("S%d_%d" % (l, h)) for h in range(8)] for l in range(NL)]
        pcarry = sb("pcarry", [128, 8, 16], F32)
        R_pc = [Res("pc%d" % i) for i in range(8)]
        hcarry = sb("hcarry", [128, 2 * NJ, 2], F32)
        R_hc = [Res("hc%d" % i) for i in range(2 * NJ)]
        big = sb("big", [128, NJ * TB], BF16)
        R_big = [Res("big%d" % i) for i in range(NJ)]
        pre_es = ExitStack()
        sgw_f = sb("sgw_f", [128, NL, 8 * 128], F32, pre_es)
        psum = [es.enter_context(nc.psum_tensor("ps%d" % i, [128, 512], F32)) for i in range(8)]
        R_ps = [Res("ps%d" % i) for i in range(8)]
        pstate = {"i": 0, "reserved": set()}

        def bank():
            while True:
                i = pstate["i"]
                pstate["i"] = (i + 1) % 8
                if i not in pstate["reserved"]:
                    return psum[i], R_ps[i]

        def bigc(i):
            return big[:, i * TB:(i + 1) * TB]

        sch.dma("sp", st_misc, vecs[:], vecs_d, writes=(R_vecs,))
        sch.dma("sp", st_misc, consts[:], consts_d, writes=(R_consts,))
        for l in range(NL):
            sch.dma("sp", st_misc, sgw_f[:, l, :].rearrange("p (g t) -> p g t", t=128), sgwT_d[l], writes=(R_sgw,))
        st_c = sch.stream("cast0")
        sch.dma("pool", st_c, ident_b[:], consts_d[:, 0:128], writes=(R_cb,))
        sch.dma("pool", st_c, sgb[:], sgb_d, writes=(R_sgb,))
        for l in range(NL):
            sch.dma("pool", st_c, poolw[:, l, :].rearrange("p (g cc d) -> p g cc d", g=4, cc=2),
                    poolw_d[l].rearrange("g (cc p) d -> p g cc d", p=128), writes=(R_poolw,))
        for e in ("act", "dve", "pe", "pool"):
            sch.wait_all(e, [st_misc, st_c])
        for r in (R_vecs, R_consts, R_sgw, R_cb, R_sgb, R_poolw):
            r.w = None
            r.r = []
        skip_pre = debug_stop == "X0a"
        _real_op = sch.op
        if skip_pre:
            sch.op = lambda *a, **k: None
        sch.op("dve", lambda e: e.memset(ones_b[:], 1.0), writes=(R_cb,))
        sch.op("dve", lambda e: e.memset(S_st[:], 0.0), writes=[r for rr in R_S for r in rr])
        sch.op("dve", lambda e: e.memset(pcarry[:], 0.0), writes=R_pc)
        sch.op("dve", lambda e: e.memset(hcarry[:], 0.0), writes=R_hc)
        sch.op("dve", lambda e: e.memset(lb1[:], 0.0), writes=(R_lb,))
        sch.op("dve", lambda e: e.tensor_sub(out=lb1[:, 0, :], in0=vecs[:, V_LB + 8:V_LB + 16], in1=vecs[:, V_LB:V_LB + 8]),
               reads=(R_vecs,), writes=(R_lb,))
        sch.op("act", lambda e: e.activation(out=lb1[:, 0, :], in_=lb1[:, 0, :], func=AF.Sigmoid), writes=(R_lb,))
        sch.op("dve", lambda e: e.tensor_scalar(out=lb1[:, 0, :], in0=lb1[:, 0, :], scalar1=vecs[:, V_FL + 1:V_FL + 2], scalar2=None,
                                                op0=ALU.mult), reads=(R_vecs,), writes=(R_lb,))
        sch.op("dve", lambda e: e.memset(xT[:], 0.0), writes=R_xT)
        xTz = xT[:].rearrange("p a b -> p (a b)").rearrange("p (tt d) -> p tt d", tt=4)
        for hf in range(2):
            sch.dma("sp", st_cs[hf], cc_dst.ap()[hf, 0:256, :].rearrange("(tt p) d -> p tt d", p=128), xTz[:, 0:2, :], reads=R_xT,
                    writes=(R_ccdst[hf],))
        sch.op("dve", lambda e: e.tensor_scalar(out=oml[:], in0=lb1[:], scalar1=-1.0, scalar2=1.0, op0=ALU.mult, op1=ALU.add),
               writes=(R_lb,))
        sch.op("dve", lambda e: e.tensor_scalar(out=noml[:], in0=oml[:], scalar1=-1.0, scalar2=None, op0=ALU.mult),
               writes=(R_lb,))
        for l in range(NL):
            for g in range(8):
                sch.op("dve", lambda e, l=l, g=g: e.tensor_mul(out=sgw[:, l, g * 128:(g + 1) * 128],
                                                                in0=sgw_f[:, l, g * 128:(g + 1) * 128], in1=tril_f),
                       reads=(R_sgw, R_consts), writes=(R_sgw,))
        sch.op = _real_op
        sch.barrier()
        pre_es.close()

        def vcol(off, n=1):
            return vecs[:, off:off + n]

        def mm_group(out_ap, out_res, pairs, reads):
            def fn(e):
                inst = None
                n = len(pairs)
                for i, (a, b) in enumerate(pairs):
                    inst = e.matmul(out_ap, a, b, start=(i == 0), stop=(i == n - 1))
                return inst
            sch.op("pe", fn, reads=reads, writes=(out_res,))

        def load_block(step):
            tb = min(step, NTB - 1)
            isA = vecs[:, V_XA + step:V_XA + step + 1]
            isB = vecs[:, V_FL + 1:V_FL + 2]
            with ExitStack() as st:
                xin = sb("xin", [128, 4, D], F32, st)
                gin = [sb("gin%d" % i, [128, D], F32, st) for i in range(2)]
                R_xin = [Res("xin%d" % i) for i in range(4)]
                R_gin = [Res("gin%d" % i) for i in range(2)]
                for tt in range(4):
                    sch.dma("sp", st_xs[tt], xin[:, tt, :], x_d[tb * TB + tt * 128:tb * TB + (tt + 1) * 128, :], writes=(R_xin[tt],))
                for tt in range(4):
                    sch.dma("sp", st_g[tt % 2], gin[tt % 2][:], cc_dst.ap()[tt // 2, (tt % 2) * 128:(tt % 2 + 1) * 128, :], reads=(R_ccdst[tt // 2],), writes=(R_gin[tt % 2],))
                    sch.op("dve", lambda e, tt=tt: e.tensor_scalar(out=xin[:, tt, :], in0=xin[:, tt, :], scalar1=isA, scalar2=None, op0=ALU.mult),
                           reads=(R_vecs,), writes=(R_xin[tt],))
                    sch.op("dve", lambda e, tt=tt: e.scalar_tensor_tensor(out=xin[:, tt, :], in0=gin[tt % 2][:], scalar=isB, in1=xin[:, tt, :],
                                                                         op0=ALU.mult, op1=ALU.add),
                           reads=(R_vecs, R_gin[tt % 2]), writes=(R_xin[tt],))
                for dc in range(KC):
                    ps, rps = bank()

                    def fn(e, dc=dc, ps=ps):
                        inst = None
                        for tt in range(4):
                            inst = e.transpose(out=ps[:, tt * 128:(tt + 1) * 128], in_=xin[:, tt, dc * 128:(dc + 1) * 128],
                                               identity=ident_f)
                        return inst
                    sch.op("pe", fn, reads=R_xin + [R_consts], writes=(rps,))
                    sch.op("act", lambda e, dc=dc, ps=ps: e.copy(out=xT[:, dc, :], in_=ps[:]), reads=(rps,), writes=(R_xT[dc],))
                    sch.op("dve", lambda e, dc=dc: e.tensor_copy(out=xTb[:, dc, :], in_=xT[:, dc, :]), reads=(R_xT[dc],),
                           writes=(R_xTb[dc],))
                sch.wait_all("sp", st_xs + st_g)
                sch.barrier()

        def store_block(tb):
            xo = big[:, 0:4 * D * 2].bitcast(F32).rearrange("p (tt d) -> p tt d", tt=4)
            ob = (tb - 1) % NTB
            for tt in range(4):
                for dq in range(4):
                    ps, rps = bank()

                    def fn(e, tt=tt, dq=dq, ps=ps):
                        inst = None
                        for i in range(4):
                            dc = dq * 4 + i
                            inst = e.transpose(out=ps[:, i * 128:(i + 1) * 128], in_=xT[:, dc, tt * 128:(tt + 1) * 128],
                                               identity=ident_f)
                        return inst
                    sch.op("pe", fn, reads=[R_xT[dq * 4 + i] for i in range(4)] + [R_consts], writes=(rps,))
                    rb_ = R_big[tt * 8 + dq * 2:tt * 8 + dq * 2 + 2]
                    if (tt * 4 + dq) % 2 == 0:
                        sch.op("act", lambda e, tt=tt, dq=dq, ps=ps: e.copy(out=xo[:, tt, dq * 512:(dq + 1) * 512], in_=ps[:]),
                               reads=(rps,), writes=rb_)
                    else:
                        sch.op("dve", lambda e, tt=tt, dq=dq, ps=ps: e.tensor_copy(out=xo[:, tt, dq * 512:(dq + 1) * 512], in_=ps[:]),
                               reads=(rps,), writes=rb_)
                if tt % 2 == 1:
                    hf = tt // 2
                    rbh = R_big[hf * 16:(hf + 1) * 16]
                    sch.dma("sp", st_outs[hf], out_d[ob * TB + hf * 256:ob * TB + (hf + 1) * 256, :].rearrange("(tt p) d -> p tt d", p=128),
                            xo[:, hf * 2:hf * 2 + 2, :], reads=rbh)
                    if tb < NSTEP - 1:
                        sch.dma("sp", st_cs[hf], cc_src.ap()[hf * 256:(hf + 1) * 256, :].rearrange("(tt p) d -> p tt d", p=128),
                                xo[:, hf * 2:hf * 2 + 2, :], reads=rbh + [R_ccsrc[hf]], writes=(R_ccsrc[hf],))
                        sch._wait("pool", sch._deps((R_ccsrc[hf],), (R_ccdst[hf],)))
                        inst = nc.gpsimd.collective_compute("AllGather", ALU.bypass, replica_groups=rgroups,
                                                            ins=[cc_src.ap()[hf * 256:(hf + 1) * 256, :].opt()],
                                                            outs=[cc_dst.ap()[hf].opt()])
                        sch.cnt["cc"] += 1
                        inst.then_inc(sch.sem["cc"], 1)
                        sch._mark(("cc", sch.cnt["cc"]), (R_ccsrc[hf],), (R_ccdst[hf],))

        def dbg_dump(ap_list):
            with ExitStack() as st:
                stg = sb("dbgstg", [128, 16 * TB], F32, st)
                R = Res("dbg")
                off = 0
                sch.barrier()
                sch.op("dve", lambda e: e.memset(stg[:], 0.0), writes=(R,))
                for ap in ap_list:
                    n = ap.shape[-1]
                    p = ap.shape[0]
                    sch.op("dve", lambda e, ap=ap, off=off, n=n, p=p: e.tensor_copy(out=stg[0:p, off:off + n], in_=ap), writes=(R,))
                    off += n
                sch.barrier()
                sch.dma("sp", st_out, dbg_d, stg[:], reads=(R,))
                sch.wait_all("sp", [st_out])
                sch.barrier()

        def mixer(l, tb):
            par = tb % 2
            xb_all = R_xTb
            y_hg = [bigc(h) for h in range(8)]
            y_pool = [bigc(8 + h) for h in range(8)]
            y_sg = [bigc(16 + h) for h in range(8)]
            R_yhg = R_big[0:8]
            R_ypool = R_big[8:16]
            R_ysg = R_big[16:24]


            with ExitStack() as st:
                def tmp(name, shape, dt=F32):
                    return sb(name, shape, dt, st)
                qs = [tmp("qs%d" % i, [128, TB]) for i in range(2)]
                bb = [tmp("bb%d" % i, [128, TB]) for i in range(2)]
                kk = [tmp("kk%d" % i, [128, TB]) for i in range(2)]
                eb = [tmp("eb%d" % i, [128, TB]) for i in range(2)]
                enb = [tmp("enb%d" % i, [128, TB]) for i in range(2)]
                rstd = enb
                Qp = [tmp("Qp%d" % i, [128, TB], BF16) for i in range(2)]
                Kp = [tmp("Kp%d" % i, [128, TB], BF16) for i in range(2)]
                sog = [tmp("sog%d" % i, [128, TB]) for i in range(2)]
                osb = kk
                vtok = tmp("vtok", [64, 8, 256], BF16)
                Sb = [tmp("Sb%d" % i, [128, 128], BF16) for i in range(2)]
                sq = Kp
                R = {n: [Res(n + str(i)) for i in range(2)] for n in
                     ("qs", "bb", "kk", "eb", "enb", "Qp", "Kp", "sog", "osb", "scm", "ktok", "Sb", "dtmp", "sq", "rstd")}
                R_vtok = Res("vtok")
                R["osb"] = R["kk"]
                R["sq"] = R["Kp"]
                R["rstd"] = R["enb"]
                scm2 = [[tmp("scm%d_%d" % (i, p), [64, 64], BF16) for p in range(2)] for i in range(2)]
                ktok2 = [[tmp("ktok%d_%d" % (i, p), [64, 128], BF16) for p in range(2)] for i in range(2)]
                dtmp2 = [[tmp("dtmp%d_%d" % (i, p), [128, 128]) for p in range(2)] for i in range(2)]
                R_scm2 = [[Res("scm2") for p in range(2)] for i in range(2)]
                R_ktok2 = [[Res("ktok2") for p in range(2)] for i in range(2)]
                R_dtmp2 = [[Res("dtmp2") for p in range(2)] for i in range(2)]

                def rms_finish(hpp):
                    for i in range(2):
                        h = hpp * 2 + i
                        ps, rps = bank()
                        mm_group(ps[:], rps, [(ones_b[:], sq[i][:])], [R_cb, R["sq"][i]])
                        sch.op("act", lambda e, i=i, ps=ps: e.activation(out=rstd[i][:], in_=ps[:], func=AF.Sqrt, scale=1.0 / 128.0,
                                                                          bias=RMS_EPS),
                               reads=(rps,), writes=(R["rstd"][i],))
                        sch.op("dve", lambda e, i=i: e.reciprocal(out=rstd[i][:], in_=rstd[i][:]), writes=(R["rstd"][i],))
                        sch.op("dve", lambda e, i=i: e.tensor_mul(out=osb[i][:], in0=osb[i][:], in1=rstd[i][:]),
                               reads=(R["rstd"][i],), writes=(R["osb"][i],))
                        sch.op("dve", lambda e, i=i, h=h: e.scalar_tensor_tensor(
                            out=y_hg[h], in0=osb[i][:], scalar=vecs[:, V_NG + l * 8 + h:V_NG + l * 8 + h + 1], in1=sog[i][:],
                            op0=ALU.mult, op1=ALU.mult),
                            reads=(R["osb"][i], R["sog"][i], R_vecs), writes=(R_yhg[h],))

                for hp in range(4):
                    w, rw = ring.acquire("q")
                    for i in range(2):
                        ps, rps = bank()
                        mm_group(ps[:], rps, [(w[:, kc, i * 128:(i + 1) * 128], xTb[:, kc, :]) for kc in range(KC)], [rw] + xb_all)
                        sch.op("act", lambda e, i=i, ps=ps: e.activation(out=qs[i][:], in_=ps[:], func=AF.Silu),
                               reads=(rps,), writes=(R["qs"][i],))
                    if hp >= 1:
                        rms_finish(hp - 1)
                    w, rw = ring.acquire("f")
                    for i in range(2):
                        h = hp * 2 + i
                        ps, rps = bank()
                        mm_group(ps[:], rps, [(w[:, kc, i * 128:(i + 1) * 128], xTb[:, kc, :]) for kc in range(KC)], [rw] + xb_all)
                        sch.op("act", lambda e, i=i, ps=ps: e.activation(out=bb[i][:], in_=ps[:], func=AF.Sigmoid),
                               reads=(rps,), writes=(R["bb"][i],))
                        sch.op("dve", lambda e, i=i, h=h: e.tensor_scalar(out=kk[i][:], in0=bb[i][:], scalar1=noml[:, l, h:h + 1],
                                                                           scalar2=oml[:, l, h:h + 1], op0=ALU.mult, op1=ALU.add),
                               reads=(R["bb"][i], R_lb), writes=(R["kk"][i],))
                        sch.op("act", lambda e, i=i, h=h: e.activation(out=bb[i][:], in_=bb[i][:], func=AF.Ln,
                                                                        scale=oml[:, l, h:h + 1], bias=lb1[:, l, h:h + 1]),
                               reads=(R["bb"][i], R_lb), writes=(R["bb"][i],))
                        sch.op("dve", lambda e, i=i: e.tensor_tensor_scan(out=eb[i][:], data0=scanmask, data1=bb[i][:], initial=0.0,
                                                                         op0=ALU.mult, op1=ALU.add),
                               reads=(R["bb"][i], R_consts), writes=(R["eb"][i],))
                        sch.op("act", lambda e, i=i: e.activation(out=enb[i][:], in_=eb[i][:], func=AF.Exp, scale=-1.0),
                               reads=(R["eb"][i],), writes=(R["enb"][i],))
                        sch.op("act", lambda e, i=i: e.activation(out=eb[i][:], in_=eb[i][:], func=AF.Exp),
                               reads=(R["eb"][i],), writes=(R["eb"][i],))
                        sch.op("dve", lambda e, i=i: e.tensor_mul(out=Qp[i][:], in0=qs[i][:], in1=eb[i][:]),
                               reads=(R["qs"][i], R["eb"][i]), writes=(R["Qp"][i],))
                        sch.op("dve", lambda e, i=i: e.tensor_mul(out=Kp[i][:], in0=kk[i][:], in1=enb[i][:]),
                               reads=(R["kk"][i], R["enb"][i]), writes=(R["Kp"][i],))
                    w, rw = ring.acquire("i")
                    for c in range(8):
                        ps, rps = bank()
                        mm_group(ps[0:64, 0:256], rps, [(xTb[:, kc, c * 64:(c + 1) * 64], w[:, kc, :]) for kc in range(KC)],
                                 [rw] + xb_all)
                        if c % 2 == 0:
                            sch.op("act", lambda e, c=c, ps=ps: e.copy(out=vtok[:, c, :], in_=ps[0:64, 0:256]), reads=(rps,),
                                   writes=(R_vtok,))
                        else:
                            sch.op("dve", lambda e, c=c, ps=ps: e.tensor_copy(out=vtok[:, c, :], in_=ps[0:64, 0:256]), reads=(rps,),
                                   writes=(R_vtok,))
                    w, rw = ring.acquire("og")
                    for i in range(2):
                        ps, rps = bank()
                        mm_group(ps[:], rps, [(w[:, kc, i * 128:(i + 1) * 128], xTb[:, kc, :]) for kc in range(KC)], [rw] + xb_all)
                        sch.op("act", lambda e, i=i, ps=ps: e.activation(out=sog[i][:], in_=ps[:], func=AF.Silu),
                               reads=(rps,), writes=(R["sog"][i],))
                    for i in range(2):
                        h = hp * 2 + i
                        sch.op("act", lambda e, i=i, h=h: e.copy(out=Sb[i][:], in_=S_st[:, l * 8 + h, :]),
                               reads=(R_S[l][h],), writes=(R["Sb"][i],))

                    def indep_a(c, i):
                        p = c % 2
                        cs = c * 64
                        ps1, rps1 = bank()
                        mm_group(ps1[0:64, 0:64], rps1, [(Kp[i][:, cs:cs + 64], Qp[i][:, cs:cs + 64])],
                                 [R["Kp"][i], R["Qp"][i]])
                        sch.op("dve", lambda e: e.tensor_mul(out=scm2[i][p][:], in0=ps1[0:64, 0:64], in1=tril_f[0:64, 0:64]),
                               reads=(rps1, R_consts), writes=(R_scm2[i][p],))
                        ps2, rps2 = bank()
                        ps2b = ps2[:].bitcast(BF16)
                        sch.op("pe", lambda e: e.transpose(out=ps2b[0:64, 0:128], in_=Kp[i][:, cs:cs + 64], identity=ident_b[:]),
                               reads=(R["Kp"][i], R_cb), writes=(rps2,))
                        sch.op("act", lambda e: e.copy(out=ktok2[i][p][:], in_=ps2b[0:64, 0:128]),
                               reads=(rps2,), writes=(R_ktok2[i][p],))

                    def indep_b(c, i):
                        p = c % 2
                        cs = c * 64
                        ps4, rps4 = bank()
                        mm_group(ps4[:, 0:128], rps4, [(ktok2[i][p][:], vtok[:, c, i * 128:(i + 1) * 128])],
                                 [R_ktok2[i][p], R_vtok])
                        gcol = eb[i][:, cs + 63:cs + 64]
                        sch.op("dve", lambda e: e.tensor_scalar(out=dtmp2[i][p][:], in0=ps4[:, 0:128], scalar1=gcol, scalar2=None, op0=ALU.mult),
                               reads=(rps4, R["eb"][i]), writes=(R_dtmp2[i][p],))

                    def dep(c, i):
                        p = c % 2
                        cs = c * 64
                        h = hp * 2 + i
                        ps3, rps3 = bank()
                        mm_group(ps3[:, 0:64], rps3,
                                 [(vtok[:, c, i * 128:(i + 1) * 128], scm2[i][p][:]), (Sb[i][:], Qp[i][:, cs:cs + 64])],
                                 [R_vtok, R_scm2[i][p], R["Sb"][i], R["Qp"][i]])
                        sch.op("act", lambda e: e.copy(out=osb[i][:, cs:cs + 64], in_=ps3[:, 0:64]),
                               reads=(rps3,), writes=(R["osb"][i],))
                        gcol = eb[i][:, cs + 63:cs + 64]
                        sch.op("dve", lambda e: e.scalar_tensor_tensor(
                            out=S_st[:, l * 8 + h, :], in0=S_st[:, l * 8 + h, :], scalar=gcol, in1=dtmp2[i][p][:],
                            op0=ALU.mult, op1=ALU.add),
                            reads=(R_dtmp2[i][p], R["eb"][i], R_S[l][h]), writes=(R_S[l][h],))
                        if c < 7:
                            sch.op("act", lambda e: e.copy(out=Sb[i][:], in_=S_st[:, l * 8 + h, :]),
                                   reads=(R_S[l][h],), writes=(R["Sb"][i],))

                    for i in range(2):
                        indep_a(0, i)
                    for i in range(2):
                        indep_b(0, i)
                    for c in range(8):
                        if c < 7:
                            for i in range(2):
                                indep_a(c + 1, i)
                        for i in range(2):
                            dep(c, i)
                        if c < 7:
                            for i in range(2):
                                indep_b(c + 1, i)
                    for i in range(2):
                        sch.op("act", lambda e, i=i: e.activation(out=sq[i][:], in_=osb[i][:], func=AF.Square),
                               reads=(R["osb"][i],), writes=(R["sq"][i],))
                    if hp == 3:
                        rms_finish(3)
                sch.barrier()

            with ExitStack() as st:
                pbuf = [sb("pbuf%d" % i, [128, 16 + TB], F32, st) for i in range(2)]
                pa = [sb("pa%d" % i, [128, 16 + TB], F32, st) for i in range(2)]
                pb_ = [sb("pb%d" % i, [128, 16 + TB], F32, st) for i in range(2)]
                pooled = [sb("pooled%d" % i, [128, TB], BF16, st) for i in range(2)]
                Rp = {n: [Res(n + str(i)) for i in range(2)] for n in ("pbuf", "pa", "pb", "pooled")}
                for i in range(2):
                    sch.op("dve", lambda e, i=i: e.memset(pa[i][:, 0:16], 0.0), writes=(Rp["pa"][i],))
                    sch.op("dve", lambda e, i=i: e.memset(pb_[i][:, 0:16], 0.0), writes=(Rp["pb"][i],))
                for g in range(4):
                    wdw = 2 ** (g + 1)
                    w, rw = ring.acquire("p")
                    for i in range(2):
                        ps, rps = bank()
                        mm_group(ps[:], rps, [(w[:, kc, i * 128:(i + 1) * 128], xTb[:, kc, :]) for kc in range(KC)], [rw] + xb_all)
                        sch.op("act", lambda e, i=i, ps=ps: e.copy(out=pbuf[i][:, 16:16 + TB], in_=ps[:]), reads=(rps,),
                               writes=(Rp["pbuf"][i],))
                        pci = g * 2 + i
                        sch.op("act", lambda e, i=i, pci=pci: e.copy(out=pbuf[i][:, 0:16], in_=pcarry[:, pci, :]), reads=(R_pc[pci],),
                               writes=(Rp["pbuf"][i],))
                        sch.op("pool", lambda e, i=i, pci=pci: e.tensor_copy(out=pcarry[:, pci, :], in_=pbuf[i][:, TB:TB + 16]),
                               reads=(Rp["pbuf"][i],), writes=(R_pc[pci],))
                        src, rsrc = pbuf[i], Rp["pbuf"][i]
                        dsts = [(pa[i], Rp["pa"][i]), (pb_[i], Rp["pb"][i])]
                        sh = 1
                        k = 0
                        N = 16 + TB
                        while sh < wdw:
                            dst, rdst = dsts[k % 2]
                            sch.op("dve", lambda e, src=src, dst=dst, sh=sh, N=N: e.tensor_add(out=dst[:, sh:N], in0=src[:, sh:N],
                                                                                                in1=src[:, 0:N - sh]),
                                   reads=(rsrc,), writes=(rdst,))
                            src, rsrc = dst, rdst
                            sh *= 2
                            k += 1
                        if tb in (0, 1):
                            corr = corr_tabs[tb]
                            sch.op("dve", lambda e, src=src, g=g, corr=corr: e.tensor_mul(out=src[:, 16:32], in0=src[:, 16:32],
                                                                                          in1=corr[:, g * 16:(g + 1) * 16]),
                                   reads=(R_consts,), writes=(rsrc,))
                        sch.op("dve", lambda e, src=src, i=i, wdw=wdw: e.scalar_tensor_tensor(
                            out=pooled[i][:], in0=src[:, 16:16 + TB], scalar=1.0 / wdw, in1=pbuf[i][:, 16:16 + TB],
                            op0=ALU.mult, op1=ALU.subtract),
                            reads=(rsrc, Rp["pbuf"][i]), writes=(Rp["pooled"][i],))
                    for dch in range(2):
                        ps, rps = bank()
                        pw = poolw[:, l, :].rearrange("p (g cc d) -> p g cc d", g=4, cc=2)
                        mm_group(ps[:], rps, [(pw[:, g, cc, dch * 128:(dch + 1) * 128], pooled[cc][:]) for cc in range(2)],
                                 [R_poolw, Rp["pooled"][0], Rp["pooled"][1]])
                        hh = g * 2 + dch
                        sch.op("act", lambda e, ps=ps, hh=hh: e.activation(out=y_pool[hh], in_=ps[:], func=AF.Identity,
                                                                            scale=vecs[:, V_PS + l * 8 + hh:V_PS + l * 8 + hh + 1]),
                               reads=(rps, R_vecs), writes=(R_ypool[hh],))
                sch.barrier()

            with ExitStack() as st:
                gu = [sb("gu%d" % i, [128, TB], F32, st) for i in range(8)]
                R_gu = [Res("gu%d" % i) for i in range(8)]
                vg = [sb("vg%d" % i, [128, 1024], F32, st) for i in range(4)]
                R_vg = [Res("vg%d" % i) for i in range(4)]
                vln = [sb("vln%d" % i, [128, 1024], BF16, st) for i in range(4)]
                R_vln = [Res("vln%d" % i) for i in range(4)]
                stats = sb("stats", [128, 4, 2, 6], F32, st)
                mv = sb("mv", [128, 4, 2], F32, st)
                sd = sb("sd", [128, 4], F32, st)
                R_st = [Res("st%d" % i) for i in range(4)]
                lng = sb("sglng", [128, 1024], F32, st)
                lnb = sb("sglnb", [128, 1024], F32, st)
                R_ln = Res("sgln")
                sch.dma("sp", st_ln, lng[:], sglng_d[l:l + 1, :].broadcast_to([128, 1024]), writes=(R_ln,))
                sch.dma("sp", st_ln, lnb[:], sglnb_d[l:l + 1, :].broadcast_to([128, 1024]), writes=(R_ln,))
                sch.wait_all("dve", [st_ln])
                R_ln.w = None
                for s4 in range(4):
                    w, rw = ring.acquire("v")
                    for tt in range(4):
                        ps, rps = bank()
                        mm_group(ps[:, 0:256], rps, [(xTb[:, kc, tt * 128:(tt + 1) * 128], w[:, kc, :]) for kc in range(KC)],
                                 [rw] + xb_all)
                        sch.op("act", lambda e, ps=ps, tt=tt, s4=s4: e.activation(out=vg[tt][:, s4 * 256:(s4 + 1) * 256], in_=ps[:, 0:256],
                                                                                  func=AF.Gelu_apprx_tanh),
                               reads=(rps,), writes=(R_vg[tt],))
                for tt in range(4):
                    for hf in range(2):
                        sch.op("dve", lambda e, tt=tt, hf=hf: e.bn_stats(out=stats[:, tt, hf, :], in_=vg[tt][:, hf * 512:(hf + 1) * 512]),
                               reads=(R_vg[tt],), writes=(R_st[tt],))
                    sch.op("dve", lambda e, tt=tt: e.bn_aggr(out=mv[:, tt, :], in_=stats[:, tt, :, :].rearrange("p a b -> p (a b)")),
                           writes=(R_st[tt],))
                    sch.op("act", lambda e, tt=tt: e.activation(out=sd[:, tt:tt + 1], in_=mv[:, tt, 1:2], func=AF.Sqrt, bias=LN_EPS),
                           writes=(R_st[tt],))
                    sch.op("dve", lambda e, tt=tt: e.reciprocal(out=sd[:, tt:tt + 1], in_=sd[:, tt:tt + 1]), writes=(R_st[tt],))
                    sch.op("dve", lambda e, tt=tt: e.tensor_scalar(out=vg[tt][:], in0=vg[tt][:], scalar1=mv[:, tt, 0:1], scalar2=sd[:, tt:tt + 1],
                                                                    op0=ALU.subtract, op1=ALU.mult),
                           reads=(R_st[tt],), writes=(R_vg[tt],))
                    sch.op("dve", lambda e, tt=tt: e.tensor_mul(out=vg[tt][:], in0=vg[tt][:], in1=lng[:]),
                           reads=(R_ln,), writes=(R_vg[tt],))
                    sch.op("dve", lambda e, tt=tt: e.tensor_add(out=vln[tt][:], in0=vg[tt][:], in1=lnb[:]),
                           reads=(R_ln, R_vg[tt]), writes=(R_vln[tt],))
                for s4 in range(4):
                    w, rw = ring.acquire("u")
                    for i in range(2):
                        ps, rps = bank()
                        mm_group(ps[:], rps, [(w[:, kc, i * 128:(i + 1) * 128], xTb[:, kc, :]) for kc in range(KC)], [rw] + xb_all)
                        sch.op("act", lambda e, ps=ps, c=s4 * 2 + i: e.activation(out=gu[c][:], in_=ps[:], func=AF.Gelu_apprx_tanh),
                               reads=(rps,), writes=(R_gu[s4 * 2 + i],))
                for g in range(8):
                    ps, rps = bank()

                    def fn(e, g=g, ps=ps):
                        inst = None
                        for tt in range(4):
                            e.matmul(ps[:, tt * 128:(tt + 1) * 128], vln[tt][:, g * 128:(g + 1) * 128], sgw[:, l, g * 128:(g + 1) * 128],
                                     start=True, stop=False, skip_group_check=True)
                            inst = e.matmul(ps[:, tt * 128:(tt + 1) * 128], ones_b[0:1, 0:128],
                                            sgb[0:1, l * 1024 + g * 128:l * 1024 + (g + 1) * 128],
                                            start=False, stop=True, skip_group_check=True)
                        return inst
                    sch.op("pe", fn, reads=R_vln + [R_sgw, R_sgb, R_cb], writes=(rps,))
                    sch.op("dve", lambda e, g=g, ps=ps: e.tensor_mul(out=y_sg[g], in0=ps[:], in1=gu[g][:]),
                           reads=(rps, R_gu[g]), writes=(R_ysg[g],))
                sch.barrier()

            merged = [bigc(24 + dc) for dc in range(KC)]
            R_mg = R_big[24:40]
            ys = [(y_hg, R_yhg), (y_pool, R_ypool), (y_sg, R_ysg)]
            with ExitStack() as st:
                sg_t = [sb("sgt%d" % i, [128, TB], F32, st) for i in range(6)]
                R_sgt = [Res("sgt%d" % i) for i in range(6)]
                macc = [sb("macc%d" % i, [128, TB], F32, st) for i in range(2)]
                R_macc = [Res("macc%d" % i) for i in range(2)]
                mt = [sb("mt%d" % i, [128, TB], F32, st) for i in range(2)]
                R_mt = [Res("mt%d" % i) for i in range(2)]
                for dp in range(8):
                    for b in range(3):
                        w, rw = ring.acquire("g%d" % b)
                        for i in range(2):
                            ps, rps = bank()
                            mm_group(ps[:], rps, [(w[:, kc, i * 128:(i + 1) * 128], xTb[:, kc, :]) for kc in range(KC)], [rw] + xb_all)
                            sch.op("act", lambda e, ps=ps, b=b, i=i: e.activation(out=sg_t[b * 2 + i][:], in_=ps[:], func=AF.Sigmoid),
                                   reads=(rps,), writes=(R_sgt[b * 2 + i],))
                        w, rw = ring.acquire("P%d" % b)
                        yb, ryb = ys[b]
                        for i in range(2):
                            ps, rps = bank()
                            mm_group(ps[:], rps, [(w[:, kc, i * 128:(i + 1) * 128], yb[kc]) for kc in range(8)], [rw] + list(ryb))
                            dc = dp * 2 + i
                            if b == 0:
                                sch.op("dve", lambda e, ps=ps, i=i: e.tensor_mul(out=macc[i][:], in0=ps[:], in1=sg_t[i][:]),
                                       reads=(rps, R_sgt[i]), writes=(R_macc[i],))
                            elif b == 1:
                                sch.op("dve", lambda e, ps=ps, i=i: e.tensor_mul(out=mt[i][:], in0=ps[:], in1=sg_t[2 + i][:]),
                                       reads=(rps, R_sgt[2 + i]), writes=(R_mt[i],))
                                sch.op("pool", lambda e, i=i: e.tensor_add(out=macc[i][:], in0=macc[i][:], in1=mt[i][:]),
                                       reads=(R_mt[i],), writes=(R_macc[i],))
                            else:
                                sch.op("dve", lambda e, ps=ps, i=i: e.tensor_mul(out=mt[i][:], in0=ps[:], in1=sg_t[4 + i][:]),
                                       reads=(rps, R_sgt[4 + i]), writes=(R_mt[i],))
                                sch.op("pool", lambda e, i=i, dc=dc: e.tensor_add(out=merged[dc], in0=macc[i][:], in1=mt[i][:]),
                                       reads=(R_mt[i], R_macc[i]), writes=(R_mg[dc],))
                sch.barrier()
            L1 = ln_begin()
            for dp in range(8):
                w, rw = ring.acquire("wo")
                for i in range(2):
                    dc = dp * 2 + i
                    ps, rps = bank()
                    mm_group(ps[:], rps, [(w[:, kc, i * 128:(i + 1) * 128], merged[kc]) for kc in range(KC)], [rw] + list(R_mg))
                    sch.op("dve", lambda e, ps=ps, dc=dc: e.scalar_tensor_tensor(out=xT[:, dc, :], in0=xT[:, dc, :], scalar=ALPHA, in1=ps[:],
                                                                                op0=ALU.mult, op1=ALU.add),
                           reads=(rps,), writes=(R_xT[dc],))
                    ln_feed(L1, dc)
            ln_finish(L1, l, V_L1G, V_L1B)
            return True

        def ln_begin():
            st = ExitStack()
            L = {"st": st}
            L["rb"] = [sb("rb%d" % i, [128, TB], BF16, st) for i in range(4)]
            L["rq"] = [sb("rq%d" % i, [128, TB], BF16, st) for i in range(4)]
            L["R_rb"] = [Res("rb%d" % i) for i in range(4)]
            L["R_rq"] = [Res("rq%d" % i) for i in range(4)]
            for nm in ("mean", "ex2", "rstd", "nmr"):
                L[nm] = sb(nm, [128, TB], F32, st)
            L["tt"] = [sb("lnt%d" % i, [128, TB], F32, st) for i in range(4)]
            L["R_t"] = [Res("lnt%d" % i) for i in range(4)]
            L["R_m"] = Res("mean")
            i0 = pstate["i"]
            L["ps_s"], L["rps_s"] = bank()
            L["bi"] = [(pstate["i"] - 1) % 8]
            L["ps_q"], L["rps_q"] = bank()
            L["bi"].append((pstate["i"] - 1) % 8)
            pstate["reserved"].update(L["bi"])
            sch._wait("pe", sch._deps((), (L["rps_s"], L["rps_q"])))
            L["fed"] = []
            L["done"] = 0
            return L

        def ln_pe(L, upto):
            while L["done"] < upto:
                dc = L["fed"][L["done"]]
                j = dc % 4
                n = L["done"]

                def fn(e, j=j, n=n):
                    e.matmul(L["ps_s"][:], ones_b[:], L["rb"][j][:], start=(n == 0), stop=(n == KC - 1))
                    return e.matmul(L["ps_q"][:], ones_b[:], L["rq"][j][:], start=(n == 0), stop=(n == KC - 1))
                sch.op("pe", fn, reads=(L["R_rb"][j], L["R_rq"][j], R_cb), writes=())
                L["done"] += 1

        def ln_feed(L, dc):
            j = dc % 4
            sch.op("act", lambda e: e.copy(out=L["rb"][j][:], in_=xT[:, dc, :]), reads=(R_xT[dc],), writes=(L["R_rb"][j],))
            sch.op("act", lambda e: e.activation(out=L["rq"][j][:], in_=xT[:, dc, :], func=AF.Square), reads=(R_xT[dc],),
                   writes=(L["R_rq"][j],))
            L["fed"].append(dc)
            ln_pe(L, len(L["fed"]) - 2)

        def ln_finish(L, l, vg_off, vb_off):
            ln_pe(L, KC)
            mean, ex2, rstd, nmr, tt_, R_t, R_m = L["mean"], L["ex2"], L["rstd"], L["nmr"], L["tt"], L["R_t"], L["R_m"]
            ps_s, ps_q, rps_s, rps_q = L["ps_s"], L["ps_q"], L["rps_s"], L["rps_q"]
            rps_s.w = ("pe", sch.cnt["pe"])
            rps_s.r = []
            rps_q.w = ("pe", sch.cnt["pe"])
            rps_q.r = []
            pstate["reserved"].difference_update(L["bi"])
            sch.op("dve", lambda e: e.tensor_scalar(out=mean[:], in0=ps_s[:], scalar1=1.0 / D, scalar2=None, op0=ALU.mult),
                   reads=(rps_s,), writes=(R_m,))
            sch.op("dve", lambda e: e.tensor_scalar(out=ex2[:], in0=ps_q[:], scalar1=1.0 / D, scalar2=None, op0=ALU.mult),
                   reads=(rps_q,), writes=(R_m,))
            sch.op("dve", lambda e: e.tensor_mul(out=nmr[:], in0=mean[:], in1=mean[:]), writes=(R_m,))
            sch.op("dve", lambda e: e.tensor_sub(out=ex2[:], in0=ex2[:], in1=nmr[:]), writes=(R_m,))
            sch.op("act", lambda e: e.activation(out=rstd[:], in_=ex2[:], func=AF.Sqrt, bias=LN_EPS), writes=(R_m,))
            sch.op("dve", lambda e: e.reciprocal(out=rstd[:], in_=rstd[:]), writes=(R_m,))
            sch.op("dve", lambda e: e.scalar_tensor_tensor(out=nmr[:], in0=mean[:], scalar=-1.0, in1=rstd[:], op0=ALU.mult, op1=ALU.mult),
                   writes=(R_m,))
            for dc in range(KC):
                j = dc % 4
                sch.op("dve", lambda e, dc=dc, j=j: e.tensor_mul(out=tt_[j][:], in0=xT[:, dc, :], in1=rstd[:]),
                       reads=(R_xT[dc], R_m), writes=(R_t[j],))
                sch.op("pool" if dc % 2 == 1 else "dve", lambda e, j=j: e.tensor_add(out=tt_[j][:], in0=tt_[j][:], in1=nmr[:]),
                       reads=(R_m,), writes=(R_t[j],))
                gcol = vecs[:, vg_off + l * 16 + dc:vg_off + l * 16 + dc + 1]
                bcol = vecs[:, vb_off + l * 16 + dc:vb_off + l * 16 + dc + 1]
                sch.op("act", lambda e, dc=dc, j=j, gcol=gcol, bcol=bcol: e.activation(out=xT[:, dc, :], in_=tt_[j][:], func=AF.Identity,
                                                                                      scale=gcol, bias=bcol),
                       reads=(R_t[j], R_vecs), writes=(R_xT[dc],))
                sch.op("act", lambda e, dc=dc, j=j, gcol=gcol, bcol=bcol: e.activation(out=xTb[:, dc, :], in_=tt_[j][:], func=AF.Identity,
                                                                                      scale=gcol, bias=bcol),
                       reads=(R_t[j], R_vecs), writes=(R_xTb[dc],))
            sch.barrier()
            L["st"].close()

        def ffn(l, tb):
            par = tb % 2
            gT = [bigc(j) for j in range(NJ)]
            with ExitStack() as st:
                hbuf = [sb("hbuf%d" % i, [128, 2 + TB], F32, st) for i in range(4)]
                acc = [sb("acc%d" % i, [128, TB], F32, st) for i in range(4)]
                sa = [sb("sa%d" % i, [128, TB], F32, st) for i in range(2)]
                R_h = [Res("hbuf%d" % i) for i in range(4)]
                R_a = [Res("acc%d" % i) for i in range(4)]
                R_sa = [Res("sa%d" % i) for i in range(2)]
                for jp in range(NJ // 2):
                    for ab in range(2):
                        w, rw = ring.acquire("ua" if ab == 0 else "ub")
                        for i in range(2):
                            bi = ab * 2 + i
                            ch = (0 if ab == 0 else NJ) + jp * 2 + i
                            ps, rps = bank()
                            mm_group(ps[:], rps, [(w[:, kc, i * 128:(i + 1) * 128], xTb[:, kc, :]) for kc in range(KC)], [rw] + R_xTb)
                            sch.op("act", lambda e, ps=ps, bi=bi: e.copy(out=hbuf[bi][:, 2:2 + TB], in_=ps[:]), reads=(rps,), writes=(R_h[bi],))
                            sch.op("act", lambda e, ch=ch, bi=bi: e.copy(out=hbuf[bi][:, 0:2], in_=hcarry[:, ch, :]), reads=(R_hc[ch],),
                                   writes=(R_h[bi],))
                            sch.op("pool", lambda e, ch=ch, bi=bi: e.tensor_copy(out=hcarry[:, ch, :], in_=hbuf[bi][:, TB:TB + 2]),
                                   reads=(R_h[bi],), writes=(R_hc[ch],))
                            cw = [vecs[:, V_CW + (l * 3 + t) * 88 + ch:V_CW + (l * 3 + t) * 88 + ch + 1] for t in range(3)]
                            cb = vecs[:, V_CB + l * 88 + ch:V_CB + l * 88 + ch + 1]
                            sch.op("dve", lambda e, bi=bi, cw=cw, cb=cb: e.tensor_scalar(out=acc[bi][:], in0=hbuf[bi][:, 2:2 + TB], scalar1=cw[2],
                                                                                        scalar2=cb, op0=ALU.mult, op1=ALU.add),
                                   reads=(R_h[bi], R_vecs), writes=(R_a[bi],))
                            sch.op("dve", lambda e, bi=bi, cw=cw: e.scalar_tensor_tensor(out=acc[bi][:], in0=hbuf[bi][:, 1:1 + TB], scalar=cw[1],
                                                                                        in1=acc[bi][:], op0=ALU.mult, op1=ALU.add),
                                   reads=(R_h[bi], R_vecs), writes=(R_a[bi],))
                            sch.op("dve", lambda e, bi=bi, cw=cw: e.scalar_tensor_tensor(out=acc[bi][:], in0=hbuf[bi][:, 0:TB], scalar=cw[0],
                                                                                        in1=acc[bi][:], op0=ALU.mult, op1=ALU.add),
                                   reads=(R_h[bi], R_vecs), writes=(R_a[bi],))
                    for i in range(2):
                        j = jp * 2 + i
                        sch.op("act", lambda e, i=i: e.activation(out=sa[i][:], in_=acc[i][:], func=AF.Silu), reads=(R_a[i],), writes=(R_sa[i],))
                        sch.op("pool", lambda e, i=i, j=j: e.tensor_mul(out=gT[j], in0=sa[i][:], in1=acc[2 + i][:]),
                               reads=(R_sa[i], R_a[2 + i]), writes=(R_big[j],))
                sch.barrier()
            L2 = ln_begin()
            for dp in range(8):
                banks = [bank(), bank()]
                for half in range(4):
                    w, rw = ring.acquire("wd%d" % half)
                    for i in range(2):
                        ps, rps = banks[i]

                        def fn(e, ps=ps, i=i, half=half, w=w):
                            inst = None
                            for k in range(11):
                                inst = e.matmul(ps[:], w[:, k, i * 128:(i + 1) * 128], gT[half * 11 + k],
                                                start=(half == 0 and k == 0), stop=(half == 3 and k == 10))
                            return inst
                        sch.op("pe", fn, reads=[rw] + R_big[half * 11:(half + 1) * 11], writes=(rps,))
                for i in range(2):
                    dc = dp * 2 + i
                    ps, rps = banks[i]
                    sch.op("dve", lambda e, ps=ps, dc=dc: e.scalar_tensor_tensor(out=xT[:, dc, :], in0=xT[:, dc, :], scalar=ALPHA, in1=ps[:],
                                                                                op0=ALU.mult, op1=ALU.add),
                           reads=(rps,), writes=(R_xT[dc],))
                    ln_feed(L2, dc)
            ln_finish(L2, l, V_L2G, V_L2B)

        for step in range(NSTEP):
            par = step % 2
            if step >= 1:
                keep = vecs[:, V_FL + 2 + step:V_FL + 3 + step]
                sch.op("dve", lambda e, keep=keep: e.tensor_scalar(out=S_st[:], in0=S_st[:], scalar1=keep, scalar2=None, op0=ALU.mult),
                       reads=(R_vecs,), writes=R_S[0])
                sch.op("dve", lambda e, keep=keep: e.tensor_scalar(out=pcarry[:], in0=pcarry[:], scalar1=keep, scalar2=None, op0=ALU.mult),
                       reads=(R_vecs,), writes=R_pc)
                sch.op("dve", lambda e, keep=keep: e.tensor_scalar(out=hcarry[:], in0=hcarry[:], scalar1=keep, scalar2=None, op0=ALU.mult),
                       reads=(R_vecs,), writes=R_hc)
            load_block(step)
            mixer(0, step)
            ffn(0, step)
            store_block(step)
        sch.wait_all("sp", st_outs)
        sch.barrier(full=True)
    return nc


def host_prep(inputs, n_cores=NCORES):
    f = np.float32
    g = lambda k: np.asarray(inputs[k], dtype=f)
    real_corr = np.zeros((64,), f)
    for gi, wdw in enumerate((2, 4, 8, 16)):
        t = np.arange(16)
        real_corr[gi * 16:(gi + 1) * 16] = (wdw / np.minimum(t + 1, wdw)).astype(f)
    x = g("x")
    maps = []
    for c in range(n_cores):
        b, li = c // 2, c % 2
        order = [li, 1 - li]
        vec = np.zeros((128, NV), f)

        def put(off, arr):
            vec[:, off:off + arr.shape[1]] = arr
        put(V_LB, g("hg_lower_bounds").reshape(2, 8, 128).transpose(2, 0, 1).reshape(128, 16))
        put(V_NG, g("hg_norm_g")[order].reshape(2, 8, 128).transpose(2, 0, 1).reshape(128, 16))
        put(V_PS, g("pool_scale")[order].reshape(2, 8, 128).transpose(2, 0, 1).reshape(128, 16))
        for off, k in ((V_L1G, "ln1_g"), (V_L1B, "ln1_b"), (V_L2G, "ln2_g"), (V_L2B, "ln2_b")):
            put(off, g(k)[order].reshape(2, 16, 128).transpose(2, 0, 1).reshape(128, 32))
        put(V_CW, g("conv_w")[order].reshape(2, 3, 88, 128).transpose(3, 0, 1, 2).reshape(128, 2 * 3 * 88))
        put(V_CB, g("conv_b")[order].reshape(2, 88, 128).transpose(2, 0, 1).reshape(128, 2 * 88))
        vec[:, V_FL] = 1.0 - li
        vec[:, V_FL + 1] = float(li)
        vec[:, V_FL + 2:V_FL + 2 + NSTEP] = 1.0
        if li == 1:
            vec[:, V_FL + 2 + 1] = 0.0
        else:
            vec[:, V_XA:V_XA + NTB] = 1.0
        consts = np.zeros((128, NCONST), f)
        consts[:, 0:128] = np.eye(128, dtype=f)
        consts[:, 128:256] = np.triu(np.ones((128, 128), f))
        m = np.ones((TB,), f)
        m[::64] = 0
        consts[:, 256:256 + TB] = m[None, :]
        ones64 = np.ones((64,), f)
        consts[:, 256 + TB:256 + TB + 64] = (real_corr if li == 0 else ones64)[None, :]
        consts[:, 256 + TB + 64:256 + TB + 128] = (ones64 if li == 0 else real_corr)[None, :]
        mp = {
            "x": np.ascontiguousarray(x[b]),
            "vecs": vec,
            "consts": consts,
            "pool_w": np.ascontiguousarray(g("pool_w")[li:li + 1]),
            "sg_wT": np.ascontiguousarray(g("sg_w")[li:li + 1].transpose(0, 3, 1, 2)),
            "sg_b": np.ascontiguousarray(g("sg_b")[li].reshape(1, 1024)),
            "sg_ln_g": np.ascontiguousarray(g("sg_ln_g")[li:li + 1]),
            "sg_ln_b": np.ascontiguousarray(g("sg_ln_b")[li:li + 1]),
        }
        for k in ("w_in", "w_hg_proj", "w_pool_proj", "w_sg_proj", "w_out", "w_up", "w_down"):
            mp[k] = np.ascontiguousarray(g(k)[li:li + 1])
        maps.append(mp)
    return maps


_CACHE = {}


def kernel(**inputs):
    if "nc" not in _CACHE:
        _CACHE["nc"] = build_program(None)
    nc = _CACHE["nc"]
    in_maps = host_prep(inputs)
    res = run_bass_kernel_spmd(nc, in_maps, core_ids=list(range(NCORES)))
    return np.stack([np.asarray(res.results[2 * b + 1]["out"], dtype=np.float32) for b in range(NB)], axis=0)
```

```python
import os
import numpy as np
import concourse.bass as bass
import concourse.mybir as mybir
from concourse.bass_utils import run_bass_kernel_spmd

F32 = mybir.dt.float32
BF16 = mybir.dt.bfloat16
AF = mybir.ActivationFunctionType
ALU = mybir.AluOpType

D = 2048
S = 2048
NB = 4
DEPTH = 2
TB = 512
NTB = S // TB
NL = 1
NSTEP = NTB + 1
NCORES = 8
KC = D // 128
D_IN = 13312
D_FF = 5632
NJ = D_FF // 128
ALPHA = float((2 * DEPTH) ** 0.25)
LN_EPS = 1e-5
RMS_EPS = 1e-6
C_Q, C_F, C_I, C_OG, C_P, C_U, C_V, C_G = 0, 1024, 2048, 3072, 4096, 5120, 6144, 7168
NSLOT = 4
SLOT_ELEMS = 16 * 256

V_LB = 0
V_NG = V_LB + 16
V_PS = V_NG + 16
V_L1G = V_PS + 16
V_L1B = V_L1G + 32
V_L2G = V_L1B + 32
V_L2B = V_L2G + 32
V_CW = V_L2B + 32
V_CB = V_CW + 2 * 3 * 88
V_FL = V_CB + 2 * 88
V_XA = V_FL + 2 + NSTEP
NV = V_XA + NSTEP
NCONST = 128 + 128 + TB + 128


class Res:
    __slots__ = ("w", "r", "name")

    def __init__(self, name=""):
        self.w = None
        self.r = []
        self.name = name


class Sched:
    def __init__(self, nc, es):
        self.nc = nc
        self.eng = {"pe": nc.tensor, "act": nc.scalar, "dve": nc.vector, "pool": nc.gpsimd, "sp": nc.sync}
        self.sem = {}
        self.cnt = {}
        self.mult = {}
        for e in ("pe", "act", "dve", "pool"):
            self.sem[e] = es.enter_context(nc.semaphore("c_" + e))
            self.cnt[e] = 0
            self.mult[e] = 1
        self.seen = {e: {} for e in self.eng}
        self.es = es
        self.n_wait = 0

    def stream(self, name):
        self.sem[name] = self.es.enter_context(self.nc.semaphore("d_" + name))
        self.cnt[name] = 0
        self.mult[name] = 16
        return name

    def _deps(self, reads, writes):
        deps = {}
        for r in reads:
            if r.w is not None:
                e, t = r.w
                if deps.get(e, 0) < t:
                    deps[e] = t
        for w in writes:
            if w.w is not None:
                e, t = w.w
                if deps.get(e, 0) < t:
                    deps[e] = t
            for e, t in w.r:
                if deps.get(e, 0) < t:
                    deps[e] = t
        return deps

    def _wait(self, eng, deps):
        seen = self.seen[eng]
        for e, t in deps.items():
            if e == eng and eng == "pe":
                continue
            if seen.get(e, 0) < t:
                self.eng[eng].wait_ge(self.sem[e], t * self.mult[e])
                seen[e] = t
                self.n_wait += 1

    def _mark(self, tick, reads, writes):
        for r in reads:
            r.r.append(tick)
            if len(r.r) > 64:
                best = {}
                for e, t in r.r:
                    if best.get(e, 0) < t:
                        best[e] = t
                r.r = list(best.items())
        for w in writes:
            w.w = tick
            w.r = []

    def op(self, eng, fn, reads=(), writes=()):
        self._wait(eng, self._deps(reads, writes))
        inst = fn(self.eng[eng])
        self.cnt[eng] += 1
        inst.then_inc(self.sem[eng], 1)
        self._mark((eng, self.cnt[eng]), reads, writes)

    def dma(self, q, stream, out, in_, reads=(), writes=()):
        self._wait(q, self._deps(reads, writes))
        inst = self.eng[q].dma_start(out=out, in_=in_)
        self.cnt[stream] += 1
        inst.then_inc(self.sem[stream], 16)
        self._mark((stream, self.cnt[stream]), reads, writes)

    def barrier(self, full=False):
        for e in (("pe",) if full else ()) + ("act", "dve", "pool", "sp"):
            deps = {o: self.cnt[o] for o in ("pe", "act", "dve", "pool") if self.cnt[o] > 0}
            self._wait(e, deps)

    def wait_all(self, eng, streams):
        self._wait(eng, {s: self.cnt[s] for s in streams if self.cnt[s] > 0})


def strip_seq(l):
    seq = []
    for hp in range(4):
        for nm, c0 in (("q", C_Q), ("f", C_F), ("i", C_I), ("og", C_OG)):
            seq.append((nm, "w_in", 0, D, c0 + hp * 256, 256))
    for g in range(4):
        seq.append(("p", "w_in", 0, D, C_P + g * 256, 256))
    for s in range(4):
        seq.append(("v", "w_in", 0, D, C_V + s * 256, 256))
    for s in range(4):
        seq.append(("u", "w_in", 0, D, C_U + s * 256, 256))
    for dp in range(8):
        for b, wn in enumerate(("w_hg_proj", "w_pool_proj", "w_sg_proj")):
            seq.append(("g%d" % b, "w_in", 0, D, C_G + b * D + dp * 256, 256))
            seq.append(("P%d" % b, wn, 0, 1024, dp * 256, 256))
    for dp in range(8):
        seq.append(("wo", "w_out", 0, D, dp * 256, 256))
    for jp in range(NJ // 2):
        seq.append(("ua", "w_up", 0, D, jp * 256, 256))
        seq.append(("ub", "w_up", 0, D, D_FF + jp * 256, 256))
    for dp in range(8):
        for q4 in range(4):
            seq.append(("wd%d" % q4, "w_down", q4 * 11 * 128, 11 * 128, dp * 256, 256))
    return seq


class Ring:
    def __init__(self, sch, nc, es, wdram):
        self.sch = sch
        self.wdram = wdram
        self.slots = []
        for i in range(NSLOT):
            t = es.enter_context(nc.sbuf_tensor("wslot%d" % i, [128, SLOT_ELEMS], BF16))
            self.slots.append((t, Res("wslot%d" % i), sch.stream("ws%d" % i)))
        self.seq = []
        for step in range(NSTEP):
            for item in strip_seq(0):
                self.seq.append((0,) + item)
        self.issued = 0
        self.pos = 0

    def _issue(self, j):
        l, key, wn, r0, nr, c0, ncol = self.seq[j]
        t, res, st = self.slots[j % NSLOT]
        kc = nr // 128
        src = self.wdram[wn][l, r0:r0 + nr, c0:c0 + ncol].rearrange("(kc p) n -> p kc n", p=128)
        dst = t[:, 0:kc * ncol].rearrange("p (kc n) -> p kc n", n=ncol)
        self.sch.dma("pool", st, dst, src, reads=(), writes=(res,))

    def acquire(self, key):
        j = self.pos
        assert self.seq[j][1] == key, (self.seq[j], key)
        while self.issued < min(len(self.seq), j + NSLOT):
            self._issue(self.issued)
            self.issued += 1
        self.pos += 1
        l, key, wn, r0, nr, c0, ncol = self.seq[j]
        t, res, st = self.slots[j % NSLOT]
        kc = nr // 128
        view = t[:, 0:kc * ncol].rearrange("p (kc n) -> p kc n", n=ncol)
        return view, res


def build_program(debug_stop=None, n_cores=NCORES):
    from contextlib import ExitStack
    nc = bass.Bass("TRN2", target_bir_lowering=False)
    es = ExitStack()
    with es:
        x_d = nc.dram_tensor("x", [S, D], F32, kind="ExternalInput").ap()
        out_d = nc.dram_tensor("out", [S, D], F32, kind="ExternalOutput").ap()
        wdram = {}
        xmode = debug_stop is not None and debug_stop.startswith("X")
        for nm, shp in () if xmode else (("w_in", [NL, D, D_IN]), ("w_hg_proj", [NL, 1024, D]), ("w_pool_proj", [NL, 1024, D]),
                        ("w_sg_proj", [NL, 1024, D]), ("w_out", [NL, D, D]), ("w_up", [NL, D, 2 * D_FF]),
                        ("w_down", [NL, D_FF, D])):
            wdram[nm] = nc.dram_tensor(nm, shp, F32, kind="ExternalInput").ap()
        vecs_d = nc.dram_tensor("vecs", [128, NV], F32, kind="ExternalInput").ap()
        poolw_d = nc.dram_tensor("pool_w", [NL, 4, 256, 256], F32, kind="ExternalInput").ap()
        sgwT_d = nc.dram_tensor("sg_wT", [NL, 128, 8, 128], F32, kind="ExternalInput").ap()
        sgb_d = nc.dram_tensor("sg_b", [1, NL * 1024], F32, kind="ExternalInput").ap()
        sglng_d = nc.dram_tensor("sg_ln_g", [NL, 1024], F32, kind="ExternalInput").ap()
        sglnb_d = nc.dram_tensor("sg_ln_b", [NL, 1024], F32, kind="ExternalInput").ap()
        consts_d = nc.dram_tensor("consts", [128, NCONST], F32, kind="ExternalInput").ap()
        cc_src = nc.dram_tensor("cc_src", [TB, D], F32)
        cc_dst = nc.dram_tensor("cc_dst", [2, TB, D], F32)
        rgroups = [[2 * i, 2 * i + 1] for i in range(n_cores // 2)]
        dbg_d = None
        if debug_stop is not None:
            dbg_d = nc.dram_tensor("dbg", [128, 16 * TB], F32, kind="ExternalOutput").ap()

        sch = Sched(nc, es)
        ring = Ring(sch, nc, es, wdram) if not xmode else None
        st_misc = sch.stream("misc")
        st_x = sch.stream("xin")
        st_out = sch.stream("xout")
        st_ln = sch.stream("sgln")
        st_g = [sch.stream("gin%d" % i) for i in range(2)]
        st_xs = [sch.stream("xin%d" % i) for i in range(4)]
        st_cs = [sch.stream("ccsrc%d" % i) for i in range(2)]
        st_outs = [sch.stream("xout%d" % i) for i in range(2)]
        sch.sem["cc"] = es.enter_context(nc.semaphore("cc_done"))
        sch.cnt["cc"] = 0
        sch.mult["cc"] = 1
        R_ccsrc = [Res("ccsrc%d" % i) for i in range(2)]
        R_ccdst = [Res("ccdst%d" % i) for i in range(2)]

        uniq = {"n": 0}

        def sb(name, shape, dt=F32, stack=es):
            uniq["n"] += 1
            return stack.enter_context(nc.sbuf_tensor("%s_%d" % (name, uniq["n"]), shape, dt))

        xT = sb("xT", [128, KC, TB], F32)
        xTb = sb("xTb", [128, KC, TB], BF16)
        R_xT = [Res("xT%d" % i) for i in range(KC)]
        R_xTb = [Res("xTb%d" % i) for i in range(KC)]
        vecs = sb("vecs_sb", [128, NV], F32)
        R_vecs = Res("vecs")
        consts = sb("consts_sb", [128, NCONST], F32)
        R_consts = Res("consts")
        ident_f = consts[:, 0:128]
        tril_f = consts[:, 128:256]
        scanmask = consts[:, 256:256 + TB]
        corr_tabs = [consts[:, 256 + TB:256 + TB + 64], consts[:, 256 + TB + 64:256 + TB + 128]]
        ident_b = sb("ident_b", [128, 128], BF16)
        ones_b = sb("ones_b", [128, 128], BF16)
        R_cb = Res("constb")
        lb1 = sb("lb", [128, 2, 8], F32)
        oml = sb("oml", [128, 2, 8], F32)
        noml = sb("noml", [128, 2, 8], F32)
        R_lb = Res("lb")
        poolw = sb("poolw", [128, NL, 4 * 2 * 256], BF16)
        R_poolw = Res("poolw")
        sgw = sb("sgw", [128, NL, 8 * 128], BF16)
        R_sgw = Res("sgw")
        sgb = sb("sgb", [1, NL * 1024], BF16)
        R_sgb = Res("sgb")
        S_st = sb("S_st", [128, NL * 8, 128], F32)
        R_S = [[Res("S%d_%d" % (l, h)) for h in range(8)] for l in range(NL)]
        pcarry = sb("pcarry", [128, 8, 16], F32)
        R_pc = [Res("pc%d" % i) for i in range(8)]
        hcarry = sb("hcarry", [128, 2 * NJ, 2], F32)
        R_hc = [Res("hc%d" % i) for i in range(2 * NJ)]
        big = sb("big", [128, NJ * TB], BF16)
        R_big = [Res("big%d" % i) for i in range(NJ)]
        pre_es = ExitStack()
        sgw_f = sb("sgw_f", [128, NL, 8 * 128], F32, pre_es)
        psum = [es.enter_context(nc.psum_tensor("ps%d" % i, [128, 512], F32)) for i in range(8)]
        R_ps = [Res("ps%d" % i) for i in range(8)]
        pstate = {"i": 0, "reserved": set()}

        def bank():
            while True:
                i = pstate["i"]
                pstate["i"] = (i + 1) % 8
                if i not in pstate["reserved"]:
                    return psum[i], R_ps[i]

        def bigc(i):
            return big[:, i * TB:(i + 1) * TB]

        sch.dma("sp", st_misc, vecs[:], vecs_d, writes=(R_vecs,))
        sch.dma("sp", st_misc, consts[:], consts_d, writes=(R_consts,))
        for l in range(NL):
            sch.dma("sp", st_misc, sgw_f[:, l, :].rearrange("p (g t) -> p g t", t=128), sgwT_d[l], writes=(R_sgw,))
        st_c = sch.stream("cast0")
        sch.dma("pool", st_c, ident_b[:], consts_d[:, 0:128], writes=(R_cb,))
        sch.dma("pool", st_c, sgb[:], sgb_d, writes=(R_sgb,))
        for l in range(NL):
            sch.dma("pool", st_c, poolw[:, l, :].rearrange("p (g cc d) -> p g cc d", g=4, cc=2),
                    poolw_d[l].rearrange("g (cc p) d -> p g cc d", p=128), writes=(R_poolw,))
        for e in ("act", "dve", "pe", "pool"):
            sch.wait_all(e, [st_misc, st_c])
        for r in (R_vecs, R_consts, R_sgw, R_cb, R_sgb, R_poolw):
            r.w = None
            r.r = []
        skip_pre = debug_stop == "X0a"
        _real_op = sch.op
        if skip_pre:
            sch.op = lambda *a, **k: None
        sch.op("dve", lambda e: e.memset(ones_b[:], 1.0), writes=(R_cb,))
        sch.op("dve", lambda e: e.memset(S_st[:], 0.0), writes=[r for rr in R_S for r in rr])
        sch.op("dve", lambda e: e.memset(pcarry[:], 0.0), writes=R_pc)
        sch.op("dve", lambda e: e.memset(hcarry[:], 0.0), writes=R_hc)
        sch.op("dve", lambda e: e.memset(lb1[:], 0.0), writes=(R_lb,))
        sch.op("dve", lambda e: e.tensor_sub(out=lb1[:, 0, :], in0=vecs[:, V_LB + 8:V_LB + 16], in1=vecs[:, V_LB:V_LB + 8]),
               reads=(R_vecs,), writes=(R_lb,))
        sch.op("act", lambda e: e.activation(out=lb1[:, 0, :], in_=lb1[:, 0, :], func=AF.Sigmoid), writes=(R_lb,))
        sch.op("dve", lambda e: e.tensor_scalar(out=lb1[:, 0, :], in0=lb1[:, 0, :], scalar1=vecs[:, V_FL + 1:V_FL + 2], scalar2=None,
                                                op0=ALU.mult), reads=(R_vecs,), writes=(R_lb,))
        sch.op("dve", lambda e: e.memset(xT[:], 0.0), writes=R_xT)
        xTz = xT[:].rearrange("p a b -> p (a b)").rearrange("p (tt d) -> p tt d", tt=4)
        for hf in range(2):
            sch.dma("sp", st_cs[hf], cc_dst.ap()[hf, 0:256, :].rearrange("(tt p) d -> p tt d", p=128), xTz[:, 0:2, :], reads=R_xT,
                    writes=(R_ccdst[hf],))
        sch.op("dve", lambda e: e.tensor_scalar(out=oml[:], in0=lb1[:], scalar1=-1.0, scalar2=1.0, op0=ALU.mult, op1=ALU.add),
               writes=(R_lb,))
        sch.op("dve", lambda e: e.tensor_scalar(out=noml[:], in0=oml[:], scalar1=-1.0, scalar2=None, op0=ALU.mult),
               writes=(R_lb,))
        for l in range(NL):
            for g in range(8):
                sch.op("dve", lambda e, l=l, g=g: e.tensor_mul(out=sgw[:, l, g * 128:(g + 1) * 128],
                                                                in0=sgw_f[:, l, g * 128:(g + 1) * 128], in1=tril_f),
                       reads=(R_sgw, R_consts), writes=(R_sgw,))
        sch.op = _real_op
        sch.barrier()
        pre_es.close()

        def vcol(off, n=1):
            return vecs[:, off:off + n]

        def mm_group(out_ap, out_res, pairs, reads):
            def fn(e):
                inst = None
                n = len(pairs)
                for i, (a, b) in enumerate(pairs):
                    inst = e.matmul(out_ap, a, b, start=(i == 0), stop=(i == n - 1))
                return inst
            sch.op("pe", fn, reads=reads, writes=(out_res,))

        def load_block(step):
            tb = min(step, NTB - 1)
            isA = vecs[:, V_XA + step:V_XA + step + 1]
            isB = vecs[:, V_FL + 1:V_FL + 2]
            with ExitStack() as st:
                xin = sb("xin", [128, 4, D], F32, st)
                gin = [sb("gin%d" % i, [128, D], F32, st) for i in range(2)]
                R_xin = [Res("xin%d" % i) for i in range(4)]
                R_gin = [Res("gin%d" % i) for i in range(2)]
                for tt in range(4):
                    sch.dma("sp", st_xs[tt], xin[:, tt, :], x_d[tb * TB + tt * 128:tb * TB + (tt + 1) * 128, :], writes=(R_xin[tt],))
                for tt in range(4):
                    sch.dma("sp", st_g[tt % 2], gin[tt % 2][:], cc_dst.ap()[tt // 2, (tt % 2) * 128:(tt % 2 + 1) * 128, :], reads=(R_ccdst[tt // 2],), writes=(R_gin[tt % 2],))
                    sch.op("dve", lambda e, tt=tt: e.tensor_scalar(out=xin[:, tt, :], in0=xin[:, tt, :], scalar1=isA, scalar2=None, op0=ALU.mult),
                           reads=(R_vecs,), writes=(R_xin[tt],))
                    sch.op("dve", lambda e, tt=tt: e.scalar_tensor_tensor(out=xin[:, tt, :], in0=gin[tt % 2][:], scalar=isB, in1=xin[:, tt, :],
                                                                         op0=ALU.mult, op1=ALU.add),
                           reads=(R_vecs, R_gin[tt % 2]), writes=(R_xin[tt],))
                for dc in range(KC):
                    ps, rps = bank()

                    def fn(e, dc=dc, ps=ps):
                        inst = None
                        for tt in range(4):
                            inst = e.transpose(out=ps[:, tt * 128:(tt + 1) * 128], in_=xin[:, tt, dc * 128:(dc + 1) * 128],
                                               identity=ident_f)
                        return inst
                    sch.op("pe", fn, reads=R_xin + [R_consts], writes=(rps,))
                    sch.op("act", lambda e, dc=dc, ps=ps: e.copy(out=xT[:, dc, :], in_=ps[:]), reads=(rps,), writes=(R_xT[dc],))
                    sch.op("dve", lambda e, dc=dc: e.tensor_copy(out=xTb[:, dc, :], in_=xT[:, dc, :]), reads=(R_xT[dc],),
                           writes=(R_xTb[dc],))
                sch.wait_all("sp", st_xs + st_g)
                sch.barrier()

        def store_block(tb):
            xo = big[:, 0:4 * D * 2].bitcast(F32).rearrange("p (tt d) -> p tt d", tt=4)
            ob = (tb - 1) % NTB
            for tt in range(4):
                for dq in range(4):
                    ps, rps = bank()

                    def fn(e, tt=tt, dq=dq, ps=ps):
                        inst = None
                        for i in range(4):
                            dc = dq * 4 + i
                            inst = e.transpose(out=ps[:, i * 128:(i + 1) * 128], in_=xT[:, dc, tt * 128:(tt + 1) * 128],
                                               identity=ident_f)
                        return inst
                    sch.op("pe", fn, reads=[R_xT[dq * 4 + i] for i in range(4)] + [R_consts], writes=(rps,))
                    rb_ = R_big[tt * 8 + dq * 2:tt * 8 + dq * 2 + 2]
                    if (tt * 4 + dq) % 2 == 0:
                        sch.op("act", lambda e, tt=tt, dq=dq, ps=ps: e.copy(out=xo[:, tt, dq * 512:(dq + 1) * 512], in_=ps[:]),
                               reads=(rps,), writes=rb_)
                    else:
                        sch.op("dve", lambda e, tt=tt, dq=dq, ps=ps: e.tensor_copy(out=xo[:, tt, dq * 512:(dq + 1) * 512], in_=ps[:]),
                               reads=(rps,), writes=rb_)
                if tt % 2 == 1:
                    hf = tt // 2
                    rbh = R_big[hf * 16:(hf + 1) * 16]
                    if tb < NSTEP - 1:
                        sch.dma("sp", st_cs[hf], cc_src.ap()[hf * 256:(hf + 1) * 256, :].rearrange("(tt p) d -> p tt d", p=128),
                                xo[:, hf * 2:hf * 2 + 2, :], reads=rbh + [R_ccsrc[hf]], writes=(R_ccsrc[hf],))
                        sch._wait("pool", sch._deps((R_ccsrc[hf],), (R_ccdst[hf],)))
                        inst = nc.gpsimd.collective_compute("AllGather", ALU.bypass, replica_groups=rgroups,
                                                            ins=[cc_src.ap()[hf * 256:(hf + 1) * 256, :].opt()],
                                                            outs=[cc_dst.ap()[hf].opt()])
                        sch.cnt["cc"] += 1
                        inst.then_inc(sch.sem["cc"], 1)
                        sch._mark(("cc", sch.cnt["cc"]), (R_ccsrc[hf],), (R_ccdst[hf],))
                    sch.dma("sp", st_outs[hf], out_d[ob * TB + hf * 256:ob * TB + (hf + 1) * 256, :].rearrange("(tt p) d -> p tt d", p=128),
                            xo[:, hf * 2:hf * 2 + 2, :], reads=rbh)

        def dbg_dump(ap_list):
            with ExitStack() as st:
                stg = sb("dbgstg", [128, 16 * TB], F32, st)
                R = Res("dbg")
                off = 0
                sch.barrier()
                sch.op("dve", lambda e: e.memset(stg[:], 0.0), writes=(R,))
                for ap in ap_list:
                    n = ap.shape[-1]
                    p = ap.shape[0]
                    sch.op("dve", lambda e, ap=ap, off=off, n=n, p=p: e.tensor_copy(out=stg[0:p, off:off + n], in_=ap), writes=(R,))
                    off += n
                sch.barrier()
                sch.dma("sp", st_out, dbg_d, stg[:], reads=(R,))
                sch.wait_all("sp", [st_out])
                sch.barrier()

        def mixer(l, tb):
            par = tb % 2
            xb_all = R_xTb
            y_hg = [bigc(h) for h in range(8)]
            y_pool = [bigc(8 + h) for h in range(8)]
            y_sg = [bigc(16 + h) for h in range(8)]
            R_yhg = R_big[0:8]
            R_ypool = R_big[8:16]
            R_ysg = R_big[16:24]


            with ExitStack() as st:
                def tmp(name, shape, dt=F32):
                    return sb(name, shape, dt, st)
                qs = [tmp("qs%d" % i, [128, TB]) for i in range(2)]
                bb = [tmp("bb%d" % i, [128, TB]) for i in range(2)]
                kk = [tmp("kk%d" % i, [128, TB]) for i in range(2)]
                eb = [tmp("eb%d" % i, [128, TB]) for i in range(2)]
                enb = [tmp("enb%d" % i, [128, TB]) for i in range(2)]
                rstd = enb
                Qp = [tmp("Qp%d" % i, [128, TB], BF16) for i in range(2)]
                Kp = [tmp("Kp%d" % i, [128, TB], BF16) for i in range(2)]
                sog = [tmp("sog%d" % i, [128, TB]) for i in range(2)]
                osb = kk
                vtok = tmp("vtok", [64, 8, 256], BF16)
                Sb = [tmp("Sb%d" % i, [128, 128], BF16) for i in range(2)]
                sq = Kp
                R = {n: [Res(n + str(i)) for i in range(2)] for n in
                     ("qs", "bb", "kk", "eb", "enb", "Qp", "Kp", "sog", "osb", "scm", "ktok", "Sb", "dtmp", "sq", "rstd")}
                R_vtok = Res("vtok")
                R["osb"] = R["kk"]
                R["sq"] = R["Kp"]
                R["rstd"] = R["enb"]
                scm2 = [[tmp("scm%d_%d" % (i, p), [64, 64], BF16) for p in range(2)] for i in range(2)]
                ktok2 = [[tmp("ktok%d_%d" % (i, p), [64, 128], BF16) for p in range(2)] for i in range(2)]
                dtmp2 = [[tmp("dtmp%d_%d" % (i, p), [128, 128]) for p in range(2)] for i in range(2)]
                R_scm2 = [[Res("scm2") for p in range(2)] for i in range(2)]
                R_ktok2 = [[Res("ktok2") for p in range(2)] for i in range(2)]
                R_dtmp2 = [[Res("dtmp2") for p in range(2)] for i in range(2)]

                def rms_finish(hpp):
                    for i in range(2):
                        h = hpp * 2 + i
                        ps, rps = bank()
                        mm_group(ps[:], rps, [(ones_b[:], sq[i][:])], [R_cb, R["sq"][i]])
                        sch.op("act", lambda e, i=i, ps=ps: e.activation(out=rstd[i][:], in_=ps[:], func=AF.Sqrt, scale=1.0 / 128.0,
                                                                          bias=RMS_EPS),
                               reads=(rps,), writes=(R["rstd"][i],))
                        sch.op("dve", lambda e, i=i: e.reciprocal(out=rstd[i][:], in_=rstd[i][:]), writes=(R["rstd"][i],))
                        sch.op("dve", lambda e, i=i: e.tensor_mul(out=osb[i][:], in0=osb[i][:], in1=rstd[i][:]),
                               reads=(R["rstd"][i],), writes=(R["osb"][i],))
                        sch.op("dve", lambda e, i=i, h=h: e.scalar_tensor_tensor(
                            out=y_hg[h], in0=osb[i][:], scalar=vecs[:, V_NG + l * 8 + h:V_NG + l * 8 + h + 1], in1=sog[i][:],
                            op0=ALU.mult, op1=ALU.mult),
                            reads=(R["osb"][i], R["sog"][i], R_vecs), writes=(R_yhg[h],))

                for hp in range(4):
                    w, rw = ring.acquire("q")
                    for i in range(2):
                        ps, rps = bank()
                        mm_group(ps[:], rps, [(w[:, kc, i * 128:(i + 1) * 128], xTb[:, kc, :]) for kc in range(KC)], [rw] + xb_all)
                        sch.op("act", lambda e, i=i, ps=ps: e.activation(out=qs[i][:], in_=ps[:], func=AF.Silu),
                               reads=(rps,), writes=(R["qs"][i],))
                    if hp >= 1:
                        rms_finish(hp - 1)
                    w, rw = ring.acquire("f")
                    for i in range(2):
                        h = hp * 2 + i
                        ps, rps = bank()
                        mm_group(ps[:], rps, [(w[:, kc, i * 128:(i + 1) * 128], xTb[:, kc, :]) for kc in range(KC)], [rw] + xb_all)
                        sch.op("act", lambda e, i=i, ps=ps: e.activation(out=bb[i][:], in_=ps[:], func=AF.Sigmoid),
                               reads=(rps,), writes=(R["bb"][i],))
                        sch.op("dve", lambda e, i=i, h=h: e.tensor_scalar(out=kk[i][:], in0=bb[i][:], scalar1=noml[:, l, h:h + 1],
                                                                           scalar2=oml[:, l, h:h + 1], op0=ALU.mult, op1=ALU.add),
                               reads=(R["bb"][i], R_lb), writes=(R["kk"][i],))
                        sch.op("act", lambda e, i=i, h=h: e.activation(out=bb[i][:], in_=bb[i][:], func=AF.Ln,
                                                                        scale=oml[:, l, h:h + 1], bias=lb1[:, l, h:h + 1]),
                               reads=(R["bb"][i], R_lb), writes=(R["bb"][i],))
                        sch.op("dve", lambda e, i=i: e.tensor_tensor_scan(out=eb[i][:], data0=scanmask, data1=bb[i][:], initial=0.0,
                                                                         op0=ALU.mult, op1=ALU.add),
                               reads=(R["bb"][i], R_consts), writes=(R["eb"][i],))
                        sch.op("act", lambda e, i=i: e.activation(out=enb[i][:], in_=eb[i][:], func=AF.Exp, scale=-1.0),
                               reads=(R["eb"][i],), writes=(R["enb"][i],))
                        sch.op("act", lambda e, i=i: e.activation(out=eb[i][:], in_=eb[i][:], func=AF.Exp),
                               reads=(R["eb"][i],), writes=(R["eb"][i],))
                        sch.op("dve", lambda e, i=i: e.tensor_mul(out=Qp[i][:], in0=qs[i][:], in1=eb[i][:]),
                               reads=(R["qs"][i], R["eb"][i]), writes=(R["Qp"][i],))
                        sch.op("dve", lambda e, i=i: e.tensor_mul(out=Kp[i][:], in0=kk[i][:], in1=enb[i][:]),
                               reads=(R["kk"][i], R["enb"][i]), writes=(R["Kp"][i],))
                    w, rw = ring.acquire("i")
                    for c in range(8):
                        ps, rps = bank()
                        mm_group(ps[0:64, 0:256], rps, [(xTb[:, kc, c * 64:(c + 1) * 64], w[:, kc, :]) for kc in range(KC)],
                                 [rw] + xb_all)
                        if c % 2 == 0:
                            sch.op("act", lambda e, c=c, ps=ps: e.copy(out=vtok[:, c, :], in_=ps[0:64, 0:256]), reads=(rps,),
                                   writes=(R_vtok,))
                        else:
                            sch.op("dve", lambda e, c=c, ps=ps: e.tensor_copy(out=vtok[:, c, :], in_=ps[0:64, 0:256]), reads=(rps,),
                                   writes=(R_vtok,))
                    w, rw = ring.acquire("og")
                    for i in range(2):
                        ps, rps = bank()
                        mm_group(ps[:], rps, [(w[:, kc, i * 128:(i + 1) * 128], xTb[:, kc, :]) for kc in range(KC)], [rw] + xb_all)
                        sch.op("act", lambda e, i=i, ps=ps: e.activation(out=sog[i][:], in_=ps[:], func=AF.Silu),
                               reads=(rps,), writes=(R["sog"][i],))
                    for i in range(2):
                        h = hp * 2 + i
                        sch.op("act", lambda e, i=i, h=h: e.copy(out=Sb[i][:], in_=S_st[:, l * 8 + h, :]),
                               reads=(R_S[l][h],), writes=(R["Sb"][i],))

                    def indep_a(c, i):
                        p = c % 2
                        cs = c * 64
                        ps1, rps1 = bank()
                        mm_group(ps1[0:64, 0:64], rps1, [(Kp[i][:, cs:cs + 64], Qp[i][:, cs:cs + 64])],
                                 [R["Kp"][i], R["Qp"][i]])
                        sch.op("dve", lambda e: e.tensor_mul(out=scm2[i][p][:], in0=ps1[0:64, 0:64], in1=tril_f[0:64, 0:64]),
                               reads=(rps1, R_consts), writes=(R_scm2[i][p],))
                        ps2, rps2 = bank()
                        ps2b = ps2[:].bitcast(BF16)
                        sch.op("pe", lambda e: e.transpose(out=ps2b[0:64, 0:128], in_=Kp[i][:, cs:cs + 64], identity=ident_b[:]),
                               reads=(R["Kp"][i], R_cb), writes=(rps2,))
                        sch.op("act", lambda e: e.copy(out=ktok2[i][p][:], in_=ps2b[0:64, 0:128]),
                               reads=(rps2,), writes=(R_ktok2[i][p],))

                    def indep_b(c, i):
                        p = c % 2
                        cs = c * 64
                        ps4, rps4 = bank()
                        mm_group(ps4[:, 0:128], rps4, [(ktok2[i][p][:], vtok[:, c, i * 128:(i + 1) * 128])],
                                 [R_ktok2[i][p], R_vtok])
                        gcol = eb[i][:, cs + 63:cs + 64]
                        sch.op("dve", lambda e: e.tensor_scalar(out=dtmp2[i][p][:], in0=ps4[:, 0:128], scalar1=gcol, scalar2=None, op0=ALU.mult),
                               reads=(rps4, R["eb"][i]), writes=(R_dtmp2[i][p],))

                    def dep(c, i):
                        p = c % 2
                        cs = c * 64
                        h = hp * 2 + i
                        ps3, rps3 = bank()
                        mm_group(ps3[:, 0:64], rps3,
                                 [(vtok[:, c, i * 128:(i + 1) * 128], scm2[i][p][:]), (Sb[i][:], Qp[i][:, cs:cs + 64])],
                                 [R_vtok, R_scm2[i][p], R["Sb"][i], R["Qp"][i]])
                        sch.op("act", lambda e: e.copy(out=osb[i][:, cs:cs + 64], in_=ps3[:, 0:64]),
                               reads=(rps3,), writes=(R["osb"][i],))
                        gcol = eb[i][:, cs + 63:cs + 64]
                        sch.op("dve", lambda e: e.scalar_tensor_tensor(
                            out=S_st[:, l * 8 + h, :], in0=S_st[:, l * 8 + h, :], scalar=gcol, in1=dtmp2[i][p][:],
                            op0=ALU.mult, op1=ALU.add),
                            reads=(R_dtmp2[i][p], R["eb"][i], R_S[l][h]), writes=(R_S[l][h],))
                        if c < 7:
                            sch.op("act", lambda e: e.copy(out=Sb[i][:], in_=S_st[:, l * 8 + h, :]),
                                   reads=(R_S[l][h],), writes=(R["Sb"][i],))

                    for i in range(2):
                        indep_a(0, i)
                    for i in range(2):
                        indep_b(0, i)
                    for c in range(8):
                        if c < 7:
                            for i in range(2):
                                indep_a(c + 1, i)
                        for i in range(2):
                            dep(c, i)
                        if c < 7:
                            for i in range(2):
                                indep_b(c + 1, i)
                    for i in range(2):
                        sch.op("act", lambda e, i=i: e.activation(out=sq[i][:], in_=osb[i][:], func=AF.Square),
                               reads=(R["osb"][i],), writes=(R["sq"][i],))
                    if hp == 3:
                        rms_finish(3)
                sch.barrier()

            with ExitStack() as st:
                pbuf = [sb("pbuf%d" % i, [128, 16 + TB], F32, st) for i in range(2)]
                pa = [sb("pa%d" % i, [128, 16 + TB], F32, st) for i in range(2)]
                pb_ = [sb("pb%d" % i, [128, 16 + TB], F32, st) for i in range(2)]
                pooled = [sb("pooled%d" % i, [128, TB], BF16, st) for i in range(2)]
                Rp = {n: [Res(n + str(i)) for i in range(2)] for n in ("pbuf", "pa", "pb", "pooled")}
                for i in range(2):
                    sch.op("dve", lambda e, i=i: e.memset(pa[i][:, 0:16], 0.0), writes=(Rp["pa"][i],))
                    sch.op("dve", lambda e, i=i: e.memset(pb_[i][:, 0:16], 0.0), writes=(Rp["pb"][i],))
                for g in range(4):
                    wdw = 2 ** (g + 1)
                    w, rw = ring.acquire("p")
                    for i in range(2):
                        ps, rps = bank()
                        mm_group(ps[:], rps, [(w[:, kc, i * 128:(i + 1) * 128], xTb[:, kc, :]) for kc in range(KC)], [rw] + xb_all)
                        sch.op("act", lambda e, i=i, ps=ps: e.copy(out=pbuf[i][:, 16:16 + TB], in_=ps[:]), reads=(rps,),
                               writes=(Rp["pbuf"][i],))
                        pci = g * 2 + i
                        sch.op("act", lambda e, i=i, pci=pci: e.copy(out=pbuf[i][:, 0:16], in_=pcarry[:, pci, :]), reads=(R_pc[pci],),
                               writes=(Rp["pbuf"][i],))
                        sch.op("pool", lambda e, i=i, pci=pci: e.tensor_copy(out=pcarry[:, pci, :], in_=pbuf[i][:, TB:TB + 16]),
                               reads=(Rp["pbuf"][i],), writes=(R_pc[pci],))
                        src, rsrc = pbuf[i], Rp["pbuf"][i]
                        dsts = [(pa[i], Rp["pa"][i]), (pb_[i], Rp["pb"][i])]
                        sh = 1
                        k = 0
                        N = 16 + TB
                        while sh < wdw:
                            dst, rdst = dsts[k % 2]
                            sch.op("dve", lambda e, src=src, dst=dst, sh=sh, N=N: e.tensor_add(out=dst[:, sh:N], in0=src[:, sh:N],
                                                                                                in1=src[:, 0:N - sh]),
                                   reads=(rsrc,), writes=(rdst,))
                            src, rsrc = dst, rdst
                            sh *= 2
                            k += 1
                        if tb in (0, 1):
                            corr = corr_tabs[tb]
                            sch.op("dve", lambda e, src=src, g=g, corr=corr: e.tensor_mul(out=src[:, 16:32], in0=src[:, 16:32],
                                                                                          in1=corr[:, g * 16:(g + 1) * 16]),
                                   reads=(R_consts,), writes=(rsrc,))
                        sch.op("dve", lambda e, src=src, i=i, wdw=wdw: e.scalar_tensor_tensor(
                            out=pooled[i][:], in0=src[:, 16:16 + TB], scalar=1.0 / wdw, in1=pbuf[i][:, 16:16 + TB],
                            op0=ALU.mult, op1=ALU.subtract),
                            reads=(rsrc, Rp["pbuf"][i]), writes=(Rp["pooled"][i],))
                    for dch in range(2):
                        ps, rps = bank()
                        pw = poolw[:, l, :].rearrange("p (g cc d) -> p g cc d", g=4, cc=2)
                        mm_group(ps[:], rps, [(pw[:, g, cc, dch * 128:(dch + 1) * 128], pooled[cc][:]) for cc in range(2)],
                                 [R_poolw, Rp["pooled"][0], Rp["pooled"][1]])
                        hh = g * 2 + dch
                        sch.op("act", lambda e, ps=ps, hh=hh: e.activation(out=y_pool[hh], in_=ps[:], func=AF.Identity,
                                                                            scale=vecs[:, V_PS + l * 8 + hh:V_PS + l * 8 + hh + 1]),
                               reads=(rps, R_vecs), writes=(R_ypool[hh],))
                sch.barrier()

            with ExitStack() as st:
                gu = [sb("gu%d" % i, [128, TB], F32, st) for i in range(8)]
                R_gu = [Res("gu%d" % i) for i in range(8)]
                vg = [sb("vg%d" % i, [128, 1024], F32, st) for i in range(4)]
                R_vg = [Res("vg%d" % i) for i in range(4)]
                vln = [sb("vln%d" % i, [128, 1024], BF16, st) for i in range(4)]
                R_vln = [Res("vln%d" % i) for i in range(4)]
                stats = sb("stats", [128, 4, 2, 6], F32, st)
                mv = sb("mv", [128, 4, 2], F32, st)
                sd = sb("sd", [128, 4], F32, st)
                R_st = [Res("st%d" % i) for i in range(4)]
                lng = sb("sglng", [128, 1024], F32, st)
                lnb = sb("sglnb", [128, 1024], F32, st)
                R_ln = Res("sgln")
                sch.dma("sp", st_ln, lng[:], sglng_d[l:l + 1, :].broadcast_to([128, 1024]), writes=(R_ln,))
                sch.dma("sp", st_ln, lnb[:], sglnb_d[l:l + 1, :].broadcast_to([128, 1024]), writes=(R_ln,))
                sch.wait_all("dve", [st_ln])
                R_ln.w = None
                for s4 in range(4):
                    w, rw = ring.acquire("v")
                    for tt in range(4):
                        ps, rps = bank()
                        mm_group(ps[:, 0:256], rps, [(xTb[:, kc, tt * 128:(tt + 1) * 128], w[:, kc, :]) for kc in range(KC)],
                                 [rw] + xb_all)
                        sch.op("act", lambda e, ps=ps, tt=tt, s4=s4: e.activation(out=vg[tt][:, s4 * 256:(s4 + 1) * 256], in_=ps[:, 0:256],
                                                                                  func=AF.Gelu_apprx_tanh),
                               reads=(rps,), writes=(R_vg[tt],))
                for tt in range(4):
                    for hf in range(2):
                        sch.op("dve", lambda e, tt=tt, hf=hf: e.bn_stats(out=stats[:, tt, hf, :], in_=vg[tt][:, hf * 512:(hf + 1) * 512]),
                               reads=(R_vg[tt],), writes=(R_st[tt],))
                    sch.op("dve", lambda e, tt=tt: e.bn_aggr(out=mv[:, tt, :], in_=stats[:, tt, :, :].rearrange("p a b -> p (a b)")),
                           writes=(R_st[tt],))
                    sch.op("act", lambda e, tt=tt: e.activation(out=sd[:, tt:tt + 1], in_=mv[:, tt, 1:2], func=AF.Sqrt, bias=LN_EPS),
                           writes=(R_st[tt],))
                    sch.op("dve", lambda e, tt=tt: e.reciprocal(out=sd[:, tt:tt + 1], in_=sd[:, tt:tt + 1]), writes=(R_st[tt],))
                    sch.op("dve", lambda e, tt=tt: e.tensor_scalar(out=vg[tt][:], in0=vg[tt][:], scalar1=mv[:, tt, 0:1], scalar2=sd[:, tt:tt + 1],
                                                                    op0=ALU.subtract, op1=ALU.mult),
                           reads=(R_st[tt],), writes=(R_vg[tt],))
                    sch.op("dve", lambda e, tt=tt: e.tensor_mul(out=vg[tt][:], in0=vg[tt][:], in1=lng[:]),
                           reads=(R_ln,), writes=(R_vg[tt],))
                    sch.op("dve", lambda e, tt=tt: e.tensor_add(out=vln[tt][:], in0=vg[tt][:], in1=lnb[:]),
                           reads=(R_ln, R_vg[tt]), writes=(R_vln[tt],))
                for s4 in range(4):
                    w, rw = ring.acquire("u")
                    for i in range(2):
                        ps, rps = bank()
                        mm_group(ps[:], rps, [(w[:, kc, i * 128:(i + 1) * 128], xTb[:, kc, :]) for kc in range(KC)], [rw] + xb_all)
                        sch.op("act", lambda e, ps=ps, c=s4 * 2 + i: e.activation(out=gu[c][:], in_=ps[:], func=AF.Gelu_apprx_tanh),
                               reads=(rps,), writes=(R_gu[s4 * 2 + i],))
                for g in range(8):
                    ps, rps = bank()

                    def fn(e, g=g, ps=ps):
                        inst = None
                        for tt in range(4):
                            e.matmul(ps[:, tt * 128:(tt + 1) * 128], vln[tt][:, g * 128:(g + 1) * 128], sgw[:, l, g * 128:(g + 1) * 128],
                                     start=True, stop=False, skip_group_check=True)
                            inst = e.matmul(ps[:, tt * 128:(tt + 1) * 128], ones_b[0:1, 0:128],
                                            sgb[0:1, l * 1024 + g * 128:l * 1024 + (g + 1) * 128],
                                            start=False, stop=True, skip_group_check=True)
                        return inst
                    sch.op("pe", fn, reads=R_vln + [R_sgw, R_sgb, R_cb], writes=(rps,))
                    sch.op("dve", lambda e, g=g, ps=ps: e.tensor_mul(out=y_sg[g], in0=ps[:], in1=gu[g][:]),
                           reads=(rps, R_gu[g]), writes=(R_ysg[g],))
                sch.barrier()

            merged = [bigc(24 + dc) for dc in range(KC)]
            R_mg = R_big[24:40]
            ys = [(y_hg, R_yhg), (y_pool, R_ypool), (y_sg, R_ysg)]
            with ExitStack() as st:
                sg_t = [sb("sgt%d" % i, [128, TB], F32, st) for i in range(6)]
                R_sgt = [Res("sgt%d" % i) for i in range(6)]
                macc = [sb("macc%d" % i, [128, TB], F32, st) for i in range(2)]
                R_macc = [Res("macc%d" % i) for i in range(2)]
                mt = [sb("mt%d" % i, [128, TB], F32, st) for i in range(2)]
                R_mt = [Res("mt%d" % i) for i in range(2)]
                for dp in range(8):
                    for b in range(3):
                        w, rw = ring.acquire("g%d" % b)
                        for i in range(2):
                            ps, rps = bank()
                            mm_group(ps[:], rps, [(w[:, kc, i * 128:(i + 1) * 128], xTb[:, kc, :]) for kc in range(KC)], [rw] + xb_all)
                            sch.op("act", lambda e, ps=ps, b=b, i=i: e.activation(out=sg_t[b * 2 + i][:], in_=ps[:], func=AF.Sigmoid),
                                   reads=(rps,), writes=(R_sgt[b * 2 + i],))
                        w, rw = ring.acquire("P%d" % b)
                        yb, ryb = ys[b]
                        for i in range(2):
                            ps, rps = bank()
                            mm_group(ps[:], rps, [(w[:, kc, i * 128:(i + 1) * 128], yb[kc]) for kc in range(8)], [rw] + list(ryb))
                            dc = dp * 2 + i
                            if b == 0:
                                sch.op("dve", lambda e, ps=ps, i=i: e.tensor_mul(out=macc[i][:], in0=ps[:], in1=sg_t[i][:]),
                                       reads=(rps, R_sgt[i]), writes=(R_macc[i],))
                            elif b == 1:
                                sch.op("dve", lambda e, ps=ps, i=i: e.tensor_mul(out=mt[i][:], in0=ps[:], in1=sg_t[2 + i][:]),
                                       reads=(rps, R_sgt[2 + i]), writes=(R_mt[i],))
                                sch.op("pool", lambda e, i=i: e.tensor_add(out=macc[i][:], in0=macc[i][:], in1=mt[i][:]),
                                       reads=(R_mt[i],), writes=(R_macc[i],))
                            else:
                                sch.op("dve", lambda e, ps=ps, i=i: e.tensor_mul(out=mt[i][:], in0=ps[:], in1=sg_t[4 + i][:]),
                                       reads=(rps, R_sgt[4 + i]), writes=(R_mt[i],))
                                sch.op("pool", lambda e, i=i, dc=dc: e.tensor_add(out=merged[dc], in0=macc[i][:], in1=mt[i][:]),
                                       reads=(R_mt[i], R_macc[i]), writes=(R_mg[dc],))
                sch.barrier()
            L1 = ln_begin()
            for dp in range(8):
                w, rw = ring.acquire("wo")
                for i in range(2):
                    dc = dp * 2 + i
                    ps, rps = bank()
                    mm_group(ps[:], rps, [(w[:, kc, i * 128:(i + 1) * 128], merged[kc]) for kc in range(KC)], [rw] + list(R_mg))
                    sch.op("dve", lambda e, ps=ps, dc=dc: e.scalar_tensor_tensor(out=xT[:, dc, :], in0=xT[:, dc, :], scalar=ALPHA, in1=ps[:],
                                                                                op0=ALU.mult, op1=ALU.add),
                           reads=(rps,), writes=(R_xT[dc],))
                    ln_feed(L1, dc)
            ln_finish(L1, l, V_L1G, V_L1B)
            return True

        def ln_begin():
            st = ExitStack()
            L = {"st": st}
            L["rb"] = [sb("rb%d" % i, [128, TB], BF16, st) for i in range(4)]
            L["rq"] = [sb("rq%d" % i, [128, TB], BF16, st) for i in range(4)]
            L["R_rb"] = [Res("rb%d" % i) for i in range(4)]
            L["R_rq"] = [Res("rq%d" % i) for i in range(4)]
            for nm in ("mean", "ex2", "rstd", "nmr"):
                L[nm] = sb(nm, [128, TB], F32, st)
            L["tt"] = [sb("lnt%d" % i, [128, TB], F32, st) for i in range(4)]
            L["R_t"] = [Res("lnt%d" % i) for i in range(4)]
            L["R_m"] = Res("mean")
            i0 = pstate["i"]
            L["ps_s"], L["rps_s"] = bank()
            L["bi"] = [(pstate["i"] - 1) % 8]
            L["ps_q"], L["rps_q"] = bank()
            L["bi"].append((pstate["i"] - 1) % 8)
            pstate["reserved"].update(L["bi"])
            sch._wait("pe", sch._deps((), (L["rps_s"], L["rps_q"])))
            L["fed"] = []
            L["done"] = 0
            return L

        def ln_pe(L, upto):
            while L["done"] < upto:
                dc = L["fed"][L["done"]]
                j = dc % 4
                n = L["done"]

                def fn(e, j=j, n=n):
                    e.matmul(L["ps_s"][:], ones_b[:], L["rb"][j][:], start=(n == 0), stop=(n == KC - 1))
                    return e.matmul(L["ps_q"][:], ones_b[:], L["rq"][j][:], start=(n == 0), stop=(n == KC - 1))
                sch.op("pe", fn, reads=(L["R_rb"][j], L["R_rq"][j], R_cb), writes=())
                L["done"] += 1

        def ln_feed(L, dc):
            j = dc % 4
            sch.op("act", lambda e: e.copy(out=L["rb"][j][:], in_=xT[:, dc, :]), reads=(R_xT[dc],), writes=(L["R_rb"][j],))
            sch.op("act", lambda e: e.activation(out=L["rq"][j][:], in_=xT[:, dc, :], func=AF.Square), reads=(R_xT[dc],),
                   writes=(L["R_rq"][j],))
            L["fed"].append(dc)
            ln_pe(L, len(L["fed"]) - 2)

        def ln_finish(L, l, vg_off, vb_off):
            ln_pe(L, KC)
            mean, ex2, rstd, nmr, tt_, R_t, R_m = L["mean"], L["ex2"], L["rstd"], L["nmr"], L["tt"], L["R_t"], L["R_m"]
            ps_s, ps_q, rps_s, rps_q = L["ps_s"], L["ps_q"], L["rps_s"], L["rps_q"]
            rps_s.w = ("pe", sch.cnt["pe"])
            rps_s.r = []
            rps_q.w = ("pe", sch.cnt["pe"])
            rps_q.r = []
            pstate["reserved"].difference_update(L["bi"])
            sch.op("dve", lambda e: e.tensor_scalar(out=mean[:], in0=ps_s[:], scalar1=1.0 / D, scalar2=None, op0=ALU.mult),
                   reads=(rps_s,), writes=(R_m,))
            sch.op("dve", lambda e: e.tensor_scalar(out=ex2[:], in0=ps_q[:], scalar1=1.0 / D, scalar2=None, op0=ALU.mult),
                   reads=(rps_q,), writes=(R_m,))
            sch.op("dve", lambda e: e.tensor_mul(out=nmr[:], in0=mean[:], in1=mean[:]), writes=(R_m,))
            sch.op("dve", lambda e: e.tensor_sub(out=ex2[:], in0=ex2[:], in1=nmr[:]), writes=(R_m,))
            sch.op("act", lambda e: e.activation(out=rstd[:], in_=ex2[:], func=AF.Sqrt, bias=LN_EPS), writes=(R_m,))
            sch.op("dve", lambda e: e.reciprocal(out=rstd[:], in_=rstd[:]), writes=(R_m,))
            sch.op("dve", lambda e: e.scalar_tensor_tensor(out=nmr[:], in0=mean[:], scalar=-1.0, in1=rstd[:], op0=ALU.mult, op1=ALU.mult),
                   writes=(R_m,))
            for dc in range(KC):
                j = dc % 4
                sch.op("dve", lambda e, dc=dc, j=j: e.tensor_mul(out=tt_[j][:], in0=xT[:, dc, :], in1=rstd[:]),
                       reads=(R_xT[dc], R_m), writes=(R_t[j],))
                sch.op("pool" if dc % 2 == 1 else "dve", lambda e, j=j: e.tensor_add(out=tt_[j][:], in0=tt_[j][:], in1=nmr[:]),
                       reads=(R_m,), writes=(R_t[j],))
                gcol = vecs[:, vg_off + l * 16 + dc:vg_off + l * 16 + dc + 1]
                bcol = vecs[:, vb_off + l * 16 + dc:vb_off + l * 16 + dc + 1]
                sch.op("act", lambda e, dc=dc, j=j, gcol=gcol, bcol=bcol: e.activation(out=xT[:, dc, :], in_=tt_[j][:], func=AF.Identity,
                                                                                      scale=gcol, bias=bcol),
                       reads=(R_t[j], R_vecs), writes=(R_xT[dc],))
                sch.op("act", lambda e, dc=dc, j=j, gcol=gcol, bcol=bcol: e.activation(out=xTb[:, dc, :], in_=tt_[j][:], func=AF.Identity,
                                                                                      scale=gcol, bias=bcol),
                       reads=(R_t[j], R_vecs), writes=(R_xTb[dc],))
            sch.barrier()
            L["st"].close()

        def ffn(l, tb):
            par = tb % 2
            gT = [bigc(j) for j in range(NJ)]
            with ExitStack() as st:
                hbuf = [sb("hbuf%d" % i, [128, 2 + TB], F32, st) for i in range(4)]
                acc = [sb("acc%d" % i, [128, TB], F32, st) for i in range(4)]
                sa = [sb("sa%d" % i, [128, TB], F32, st) for i in range(2)]
                R_h = [Res("hbuf%d" % i) for i in range(4)]
                R_a = [Res("acc%d" % i) for i in range(4)]
                R_sa = [Res("sa%d" % i) for i in range(2)]
                for jp in range(NJ // 2):
                    for ab in range(2):
                        w, rw = ring.acquire("ua" if ab == 0 else "ub")
                        for i in range(2):
                            bi = ab * 2 + i
                            ch = (0 if ab == 0 else NJ) + jp * 2 + i
                            ps, rps = bank()
                            mm_group(ps[:], rps, [(w[:, kc, i * 128:(i + 1) * 128], xTb[:, kc, :]) for kc in range(KC)], [rw] + R_xTb)
                            sch.op("act", lambda e, ps=ps, bi=bi: e.copy(out=hbuf[bi][:, 2:2 + TB], in_=ps[:]), reads=(rps,), writes=(R_h[bi],))
                            sch.op("act", lambda e, ch=ch, bi=bi: e.copy(out=hbuf[bi][:, 0:2], in_=hcarry[:, ch, :]), reads=(R_hc[ch],),
                                   writes=(R_h[bi],))
                            sch.op("pool", lambda e, ch=ch, bi=bi: e.tensor_copy(out=hcarry[:, ch, :], in_=hbuf[bi][:, TB:TB + 2]),
                                   reads=(R_h[bi],), writes=(R_hc[ch],))
                            cw = [vecs[:, V_CW + (l * 3 + t) * 88 + ch:V_CW + (l * 3 + t) * 88 + ch + 1] for t in range(3)]
                            cb = vecs[:, V_CB + l * 88 + ch:V_CB + l * 88 + ch + 1]
                            sch.op("dve", lambda e, bi=bi, cw=cw, cb=cb: e.tensor_scalar(out=acc[bi][:], in0=hbuf[bi][:, 2:2 + TB], scalar1=cw[2],
                                                                                        scalar2=cb, op0=ALU.mult, op1=ALU.add),
                                   reads=(R_h[bi], R_vecs), writes=(R_a[bi],))
                            sch.op("dve", lambda e, bi=bi, cw=cw: e.scalar_tensor_tensor(out=acc[bi][:], in0=hbuf[bi][:, 1:1 + TB], scalar=cw[1],
                                                                                        in1=acc[bi][:], op0=ALU.mult, op1=ALU.add),
                                   reads=(R_h[bi], R_vecs), writes=(R_a[bi],))
                            sch.op("dve", lambda e, bi=bi, cw=cw: e.scalar_tensor_tensor(out=acc[bi][:], in0=hbuf[bi][:, 0:TB], scalar=cw[0],
                                                                                        in1=acc[bi][:], op0=ALU.mult, op1=ALU.add),
                                   reads=(R_h[bi], R_vecs), writes=(R_a[bi],))
                    for i in range(2):
                        j = jp * 2 + i
                        sch.op("act", lambda e, i=i: e.activation(out=sa[i][:], in_=acc[i][:], func=AF.Silu), reads=(R_a[i],), writes=(R_sa[i],))
                        sch.op("pool", lambda e, i=i, j=j: e.tensor_mul(out=gT[j], in0=sa[i][:], in1=acc[2 + i][:]),
                               reads=(R_sa[i], R_a[2 + i]), writes=(R_big[j],))
                sch.barrier()
            L2 = ln_begin()
            for dp in range(8):
                banks = [bank(), bank()]
                for half in range(4):
                    w, rw = ring.acquire("wd%d" % half)
                    for i in range(2):
                        ps, rps = banks[i]

                        def fn(e, ps=ps, i=i, half=half, w=w):
                            inst = None
                            for k in range(11):
                                inst = e.matmul(ps[:], w[:, k, i * 128:(i + 1) * 128], gT[half * 11 + k],
                                                start=(half == 0 and k == 0), stop=(half == 3 and k == 10))
                            return inst
                        sch.op("pe", fn, reads=[rw] + R_big[half * 11:(half + 1) * 11], writes=(rps,))
                for i in range(2):
                    dc = dp * 2 + i
                    ps, rps = banks[i]
                    sch.op("dve", lambda e, ps=ps, dc=dc: e.scalar_tensor_tensor(out=xT[:, dc, :], in0=xT[:, dc, :], scalar=ALPHA, in1=ps[:],
                                                                                op0=ALU.mult, op1=ALU.add),
                           reads=(rps,), writes=(R_xT[dc],))
                    ln_feed(L2, dc)
            ln_finish(L2, l, V_L2G, V_L2B)

        for step in range(NSTEP):
            par = step % 2
            if step >= 1:
                keep = vecs[:, V_FL + 2 + step:V_FL + 3 + step]
                sch.op("dve", lambda e, keep=keep: e.tensor_scalar(out=S_st[:], in0=S_st[:], scalar1=keep, scalar2=None, op0=ALU.mult),
                       reads=(R_vecs,), writes=R_S[0])
                sch.op("dve", lambda e, keep=keep: e.tensor_scalar(out=pcarry[:], in0=pcarry[:], scalar1=keep, scalar2=None, op0=ALU.mult),
                       reads=(R_vecs,), writes=R_pc)
                sch.op("dve", lambda e, keep=keep: e.tensor_scalar(out=hcarry[:], in0=hcarry[:], scalar1=keep, scalar2=None, op0=ALU.mult),
                       reads=(R_vecs,), writes=R_hc)
            load_block(step)
            mixer(0, step)
            ffn(0, step)
            store_block(step)
        sch.wait_all("sp", st_outs)
        sch.barrier(full=True)
    return nc


def host_prep(inputs, n_cores=NCORES):
    f = np.float32
    g = lambda k: np.asarray(inputs[k], dtype=f)
    real_corr = np.zeros((64,), f)
    for gi, wdw in enumerate((2, 4, 8, 16)):
        t = np.arange(16)
        real_corr[gi * 16:(gi + 1) * 16] = (wdw / np.minimum(t + 1, wdw)).astype(f)
    x = g("x")
    maps = []
    for c in range(n_cores):
        b, li = c // 2, c % 2
        order = [li, 1 - li]
        vec = np.zeros((128, NV), f)

        def put(off, arr):
            vec[:, off:off + arr.shape[1]] = arr
        put(V_LB, g("hg_lower_bounds").reshape(2, 8, 128).transpose(2, 0, 1).reshape(128, 16))
        put(V_NG, g("hg_norm_g")[order].reshape(2, 8, 128).transpose(2, 0, 1).reshape(128, 16))
        put(V_PS, g("pool_scale")[order].reshape(2, 8, 128).transpose(2, 0, 1).reshape(128, 16))
        for off, k in ((V_L1G, "ln1_g"), (V_L1B, "ln1_b"), (V_L2G, "ln2_g"), (V_L2B, "ln2_b")):
            put(off, g(k)[order].reshape(2, 16, 128).transpose(2, 0, 1).reshape(128, 32))
        put(V_CW, g("conv_w")[order].reshape(2, 3, 88, 128).transpose(3, 0, 1, 2).reshape(128, 2 * 3 * 88))
        put(V_CB, g("conv_b")[order].reshape(2, 88, 128).transpose(2, 0, 1).reshape(128, 2 * 88))
        vec[:, V_FL] = 1.0 - li
        vec[:, V_FL + 1] = float(li)
        vec[:, V_FL + 2:V_FL + 2 + NSTEP] = 1.0
        if li == 1:
            vec[:, V_FL + 2 + 1] = 0.0
        else:
            vec[:, V_XA:V_XA + NTB] = 1.0
        consts = np.zeros((128, NCONST), f)
        consts[:, 0:128] = np.eye(128, dtype=f)
        consts[:, 128:256] = np.triu(np.ones((128, 128), f))
        m = np.ones((TB,), f)
        m[::64] = 0
        consts[:, 256:256 + TB] = m[None, :]
        ones64 = np.ones((64,), f)
        consts[:, 256 + TB:256 + TB + 64] = (real_corr if li == 0 else ones64)[None, :]
        consts[:, 256 + TB + 64:256 + TB + 128] = (ones64 if li == 0 else real_corr)[None, :]
        mp = {
            "x": np.ascontiguousarray(x[b]),
            "vecs": vec,
            "consts": consts,
            "pool_w": np.ascontiguousarray(g("pool_w")[li:li + 1]),
            "sg_wT": np.ascontiguousarray(g("sg_w")[li:li + 1].transpose(0, 3, 1, 2)),
            "sg_b": np.ascontiguousarray(g("sg_b")[li].reshape(1, 1024)),
            "sg_ln_g": np.ascontiguousarray(g("sg_ln_g")[li:li + 1]),
            "sg_ln_b": np.ascontiguousarray(g("sg_ln_b")[li:li + 1]),
        }
        for k in ("w_in", "w_hg_proj", "w_pool_proj", "w_sg_proj", "w_out", "w_up", "w_down"):
            mp[k] = np.ascontiguousarray(g(k)[li:li + 1])
        maps.append(mp)
    return maps


_CACHE = {}


def kernel(**inputs):
    if "nc" not in _CACHE:
        _CACHE["nc"] = build_program(None)
    nc = _CACHE["nc"]
    in_maps = host_prep(inputs)
    res = run_bass_kernel_spmd(nc, in_maps, core_ids=list(range(NCORES)))
    return np.stack([np.asarray(res.results[2 * b + 1]["out"], dtype=np.float32) for b in range(NB)], axis=0)
```
